# Optimizing a Trainium2 kernel written in Bass

```python
import math
import jax
import jax.numpy as jnp
from jax import lax
import numpy as np

D_MODEL = 1024
BATCH = 16
SEQ = 2048
DEPTH = 2

CTX_LEN = 256
GRID_W = 64
N_EVEN = (DEPTH + 1) // 2
N_ODD = DEPTH // 2
ALPHA = (2.0 * DEPTH) ** 0.25
BETA = (8.0 * DEPTH) ** -0.25
N_MOD = 9
D_FF = 2816
LN_EPS = 1e-6
D_FOURIER = D_MODEL // 2
FOURIER_GROUPS = 8
D_HYENA = D_MODEL // 2
HY_ORDER = 2
HY_SHORT = 3
HY_EMB = 33
HY_HID = 64
HY_FAST_DECAY = 0.3
HY_SLOW_DECAY = 1.5
HY_TARGET = 1e-2
EV_IN = D_FOURIER + (HY_ORDER + 1) * D_HYENA
CHUNK = 64
GLA_HEADS = 4
GLA_DK = D_MODEL // 16
GLA_DV = D_MODEL // 8
GLA_RANK = 16
GLA_TAU = 16.0
D_GLA = GLA_HEADS * GLA_DV
HG_HEADS = 4
HG_D = 128
D_HG = HG_HEADS * HG_D
OD_SIZES = (GLA_HEADS * GLA_DK, GLA_HEADS * GLA_DK, D_GLA, D_GLA, GLA_RANK, GLA_RANK,
            D_HG, D_HG, D_HG, D_HG, D_HG)
OD_IN = sum(OD_SIZES)

kernel_name = 'hybrid_fourier_hyena_gla_hgrn2_prefix_dit'


def _split(p, sizes):
    idx = np.cumsum(np.array(sizes))[:-1].tolist()
    return jnp.split(p, idx, axis=-1)


def _ln(x):
    xf = x.astype(jnp.float32)
    mu = jnp.mean(xf, axis=-1, keepdims=True)
    var = jnp.mean(jnp.square(xf - mu), axis=-1, keepdims=True)
    return (xf - mu) * lax.rsqrt(var + LN_EPS)


def _modulate(x, shift, scale):
    return (_ln(x) * (1.0 + scale) + shift).astype(x.dtype)


def _post_ln(x, y, g, b):
    z = ALPHA * x.astype(jnp.float32) + y.astype(jnp.float32)
    return (_ln(z) * g + b).astype(x.dtype)


def _rms_heads(o, g):
    of = o.astype(jnp.float32)
    return of * lax.rsqrt(jnp.mean(jnp.square(of), axis=-1, keepdims=True) + LN_EPS) * g


def _ffn_sublayer(x, shift, scale, gate, w_in, w_out, g, b):
    a, u = jnp.split(_modulate(x, shift, scale) @ w_in, 2, axis=-1)
    y = (jax.nn.silu(a) * u) @ w_out
    return _post_ln(x, 0.5 * gate * y, g, b)


def _fourier_mix(a):
    B, L, C = a.shape
    ag = a.astype(jnp.float32).reshape(B, L, FOURIER_GROUPS, C // FOURIER_GROUPS)
    y = jnp.fft.fft2(ag, axes=(1, 3), norm='ortho').real
    return y.reshape(B, L, C).astype(a.dtype)


def _short_conv(u, w, b):
    up = jnp.pad(u, ((0, 0), (1, 1), (0, 0)))
    return up[:, :-2] * w[0] + up[:, 1:-1] * w[1] + up[:, 2:] * w[2] + b


def _hyena_filters(L, w1, b1, w2, b2, w3, freq):
    f32 = jnp.float32
    t = jnp.linspace(0.0, 1.0, L, dtype=f32)[:, None]
    bands = (HY_EMB - 1) // 2
    fr = jnp.linspace(1e-4, bands - 1, bands, dtype=f32)[None, :]
    idx = jnp.arange(L, dtype=f32)[:, None]
    w = 2.0 * math.pi * idx * fr / L
    z = jnp.concatenate([t, jnp.cos(w), -jnp.sin(w)], axis=-1)
    hdn = jnp.sin(freq * (z @ w1 + b1))
    hdn = jnp.sin(freq * (hdn @ w2 + b2))
    h = (hdn @ w3).astype(f32).reshape(L, 2, HY_ORDER, D_HYENA)
    deltas = jnp.abs(jnp.linspace(math.log(HY_TARGET) / HY_SLOW_DECAY,
                                  math.log(HY_TARGET) / HY_FAST_DECAY, D_HYENA, dtype=f32))
    h = h * jnp.exp(-t * deltas)[:, None, None, :]
    fwd, bwd = h[:, 0], h[:, 1]
    k = jnp.concatenate([fwd, jnp.zeros((1, HY_ORDER, D_HYENA), f32), jnp.flip(bwd[1:], axis=0)], axis=0)
    return k / (jnp.sum(jnp.abs(k), axis=0, keepdims=True) + 1e-6)


def _fft_conv(z, k):
    L = z.shape[1]
    Z = jnp.fft.rfft(z, n=2 * L, axis=1)
    K = jnp.fft.rfft(k, n=2 * L, axis=0)
    return jnp.fft.irfft(Z * K[None], n=2 * L, axis=1)[:, :L]


def _hyena(u, conv_w, conv_b, w1, b1, w2, b2, w3, freq, skip):
    L = u.shape[1]
    u = _short_conv(u, conv_w, conv_b)
    v, x1, x2 = jnp.split(u, 3, axis=-1)
    k = _hyena_filters(L, w1, b1, w2, b2, w3, freq)
    z = v.astype(jnp.float32)
    z = x1 * (_fft_conv(z, k[:, 0]) + z * skip[0])
    z = x2 * (_fft_conv(z, k[:, 1]) + z * skip[1])
    return z.astype(u.dtype)


def _even_mixer(h, w_in, w_out, conv_w, conv_b, w1, b1, w2, b2, w3, freq, skip):
    p = h @ w_in
    a, u = p[..., :D_FOURIER], p[..., D_FOURIER:]
    y = jnp.concatenate([_fourier_mix(a), _hyena(u, conv_w, conv_b, w1, b1, w2, b2, w3, freq, skip)], axis=-1)
    return y @ w_out


def _chunked_gla(q, k, v, g, s0):
    dt = v.dtype
    B, L, H, dk = q.shape
    dv = v.shape[-1]
    n = L // CHUNK
    blk = lambda a: a.astype(jnp.float32).reshape(B, n, CHUNK, H, a.shape[-1])
    q, k, v, g = blk(q), blk(k), blk(v), blk(g)
    G = jnp.cumsum(g, axis=2)
    G_last = G[:, :, -1:]
    q_in = q * jnp.exp(G)
    k_in = k * jnp.exp(-G)
    k_out = k * jnp.exp(G_last - G)
    att = jnp.einsum('bnihd,bnjhd->bnhij', q_in, k_in)
    upto_self = jnp.tril(jnp.ones((CHUNK, CHUNK), dtype=bool))
    att = jnp.where(upto_self, att, 0.0)
    o_intra = jnp.einsum('bnhij,bnjhv->bnihv', att, v)

    def step(S, inp):
        qc, kc, vc, dc = inp
        o = jnp.einsum('bihd,bhdv->bihv', qc, S)
        S = dc[..., None] * S + jnp.einsum('bjhd,bjhv->bhdv', kc, vc)
        return S, o

    xs = (jnp.moveaxis(q_in, 1, 0), jnp.moveaxis(k_out, 1, 0), jnp.moveaxis(v, 1, 0),
          jnp.moveaxis(jnp.exp(G_last[:, :, 0]), 1, 0))
    S, o_inter = lax.scan(step, s0.astype(jnp.float32), xs)
    o = o_intra + jnp.moveaxis(o_inter, 0, 1)
    return o.reshape(B, L, H, dv).astype(dt), S


def _final_state(k, v, g):
    G = jnp.cumsum(g.astype(jnp.float32), axis=1)
    w = jnp.exp(G[:, -1:] - G)
    return jnp.einsum('blhd,blhv->bhdv', k.astype(jnp.float32) * w, v.astype(jnp.float32))


def _flip(a):
    return jnp.flip(a, axis=1)


def _scan_two_way(lat, ctx, need_ctx_out):
    q, kf, gf, kb, gb, v = lat
    cq, ckf, cgf, ckb, cgb, cv = ctx
    B, _, H, dk = kf.shape
    dv = v.shape[-1]
    if need_ctx_out:
        s0 = jnp.zeros((B, H, dk, dv), jnp.float32)
        co_f, sf = _chunked_gla(cq, ckf, cv, cgf, s0)
        co_b, sb = _chunked_gla(_flip(cq), _flip(ckb), _flip(cv), _flip(cgb), s0)
        co = co_f + _flip(co_b)
    else:
        sf = _final_state(ckf, cv, cgf)
        sb = _final_state(_flip(ckb), _flip(cv), _flip(cgb))
        co = None
    o_f, _ = _chunked_gla(q, kf, v, gf, sf)
    o_b, _ = _chunked_gla(_flip(q), _flip(kb), _flip(v), _flip(gb), sb)
    return o_f + _flip(o_b), co


def _odd_mixer(h, hc, w_in, w_out, a_up, a_b, gla_g, lb, hg_g, need_ctx_out):
    B, L, _ = h.shape
    rows = L // GRID_W

    def to_col(a):
        return a.reshape((B, rows, GRID_W) + a.shape[2:]).swapaxes(1, 2).reshape(a.shape)

    def from_col(a):
        return a.reshape((B, GRID_W, rows) + a.shape[2:]).swapaxes(1, 2).reshape(a.shape)

    def features(u):
        lead = u.shape[:2]
        heads = lambda a, H: a.reshape(lead + (H, -1))
        qc, kc, vc, rc, af, ab, qd, ffl, fbl, idd, gd = _split(u @ w_in, OD_SIZES)
        q = heads(qc, GLA_HEADS) * (GLA_DK ** -0.5)
        k = heads(kc, GLA_HEADS)
        v = heads(vc, GLA_HEADS)
        gf = heads(jax.nn.log_sigmoid((af @ a_up[0] + a_b[0]).astype(jnp.float32)) / GLA_TAU, GLA_HEADS)
        gb = heads(jax.nn.log_sigmoid((ab @ a_up[1] + a_b[1]).astype(jnp.float32)) / GLA_TAU, GLA_HEADS)
        ff = lb + (1.0 - lb) * jax.nn.sigmoid(ffl.astype(jnp.float32))
        fb = lb + (1.0 - lb) * jax.nn.sigmoid(fbl.astype(jnp.float32))
        gla = (q, k, gf, k, gb, v)
        hg = (heads(qd, HG_HEADS), heads(1.0 - ff, HG_HEADS), heads(jnp.log(ff), HG_HEADS),
              heads(1.0 - fb, HG_HEADS), heads(jnp.log(fb), HG_HEADS), heads(idd, HG_HEADS))
        return gla, hg, heads(rc, GLA_HEADS), heads(gd, HG_HEADS)

    lat_gla, lat_hg, r, g = features(h)
    ctx_gla, ctx_hg, rc, gc = features(hc)
    o_gla, co_gla = _scan_two_way(lat_gla, ctx_gla, need_ctx_out)
    o_hg, co_hg = _scan_two_way(tuple(to_col(a) for a in lat_hg), ctx_hg, need_ctx_out)
    o_hg = from_col(o_hg)

    def readout(og, oh, r_, g_):
        yg = _rms_heads(og, gla_g) * jax.nn.silu(r_.astype(jnp.float32))
        yh = _rms_heads(oh.astype(jnp.float32) * jax.nn.sigmoid(g_.astype(jnp.float32)), hg_g)
        y = jnp.concatenate([yg.reshape(og.shape[:2] + (D_GLA,)), yh.reshape(oh.shape[:2] + (D_HG,))], axis=-1)
        return y.astype(h.dtype) @ w_out

    y = readout(o_gla, o_hg, r, g)
    yc = readout(co_gla, co_hg, rc, gc) if need_ctx_out else None
    return y, yc


def setup_inputs(seed: int = 0) -> dict:
    key = jax.random.key(seed)
    ks = iter(jax.random.split(key, 28))
    D = D_MODEL

    def nrm(shape, scale):
        return jax.random.normal(next(ks), shape, jnp.float32) * scale

    return {
        'x': nrm((BATCH, SEQ, D), 1.0),
        'c': nrm((BATCH, D), 1.0),
        'ctx': nrm((BATCH, CTX_LEN, D), 1.0),
        'c_ctx': nrm((D,), 1.0),
        'mod_w': nrm((DEPTH, D, N_MOD * D), D ** -0.5),
        'mod_b': nrm((DEPTH, N_MOD * D), 0.02),
        'ffn_w_in': nrm((DEPTH, 2, D, 2 * D_FF), D ** -0.5),
        'ffn_w_out': nrm((DEPTH, 2, D_FF, D), D_FF ** -0.5 * BETA),
        'ln_g': 1.0 + nrm((DEPTH, 3, D), 0.02),
        'ln_b': nrm((DEPTH, 3, D), 0.02),
        'ev_w_in': nrm((N_EVEN, D, EV_IN), D ** -0.5),
        'ev_w_out': nrm((N_EVEN, D_FOURIER + D_HYENA, D), (D_FOURIER + D_HYENA) ** -0.5 * BETA),
        'hy_conv_w': nrm((N_EVEN, HY_SHORT, 3 * D_HYENA), HY_SHORT ** -0.5),
        'hy_conv_b': nrm((N_EVEN, 3 * D_HYENA), 0.02),
        'hy_w1': nrm((N_EVEN, HY_EMB, HY_HID), HY_EMB ** -0.5),
        'hy_b1': nrm((N_EVEN, HY_HID), 0.02),
        'hy_w2': nrm((N_EVEN, HY_HID, HY_HID), HY_HID ** -0.5),
        'hy_b2': nrm((N_EVEN, HY_HID), 0.02),
        'hy_w3': nrm((N_EVEN, HY_HID, 2 * HY_ORDER * D_HYENA), HY_HID ** -0.5),
        'hy_freq': 1.0 + nrm((N_EVEN, HY_HID), 0.02),
        'hy_skip': nrm((N_EVEN, HY_ORDER, D_HYENA), 1.0),
        'od_w_in': nrm((N_ODD, D, OD_IN), D ** -0.5),
        'od_w_out': nrm((N_ODD, D_GLA + D_HG, D), (D_GLA + D_HG) ** -0.5 * BETA),
        'gla_a_up': nrm((N_ODD, 2, GLA_RANK, GLA_HEADS * GLA_DK), GLA_RANK ** -0.5),
        'gla_a_b': nrm((N_ODD, 2, GLA_HEADS * GLA_DK), 0.02),
        'gla_norm_g': 1.0 + nrm((N_ODD, GLA_DV), 0.02),
        'hg_lb': nrm((DEPTH, D_HG), 0.1),
        'hg_norm_g': 1.0 + nrm((N_ODD, HG_D), 0.02),
    }


def reference(x, c, ctx, c_ctx, mod_w, mod_b, ffn_w_in, ffn_w_out, ln_g, ln_b,
              ev_w_in, ev_w_out, hy_conv_w, hy_conv_b, hy_w1, hy_b1, hy_w2, hy_b2, hy_w3, hy_freq, hy_skip,
              od_w_in, od_w_out, gla_a_up, gla_a_b, gla_norm_g, hg_lb, hg_norm_g):
    B = x.shape[0]
    sm = jax.nn.softmax(hg_lb.astype(jnp.float32), axis=0)
    lower_bounds = jnp.cumsum(sm, axis=0) - sm[0]
    xc = ctx
    for i in range(DEPTH):
        last = i == DEPTH - 1
        odd = i % 2 == 1
        ctx_used = odd or not last
        j = i // 2
        m = (jax.nn.silu(c) @ mod_w[i] + mod_b[i]).reshape(B, N_MOD, 1, D_MODEL)
        mc = (jax.nn.silu(c_ctx) @ mod_w[i] + mod_b[i]).reshape(N_MOD, D_MODEL)
        x = _ffn_sublayer(x, m[:, 0], m[:, 1], m[:, 2], ffn_w_in[i, 0], ffn_w_out[i, 0], ln_g[i, 0], ln_b[i, 0])
        if ctx_used:
            xc = _ffn_sublayer(xc, mc[0], mc[1], mc[2], ffn_w_in[i, 0], ffn_w_out[i, 0], ln_g[i, 0], ln_b[i, 0])
        h = _modulate(x, m[:, 3], m[:, 4])
        if odd:
            hc = _modulate(xc, mc[3], mc[4])
            y, yc = _odd_mixer(h, hc, od_w_in[j], od_w_out[j], gla_a_up[j], gla_a_b[j], gla_norm_g[j],
                               lower_bounds[i], hg_norm_g[j], not last)
        else:
            ev = (ev_w_in[j], ev_w_out[j], hy_conv_w[j], hy_conv_b[j], hy_w1[j], hy_b1[j],
                  hy_w2[j], hy_b2[j], hy_w3[j], hy_freq[j], hy_skip[j])
            y = _even_mixer(h, *ev)
            yc = _even_mixer(_modulate(xc, mc[3], mc[4]), *ev) if not last else None
        x = _post_ln(x, m[:, 5] * y, ln_g[i, 1], ln_b[i, 1])
        x = _ffn_sublayer(x, m[:, 6], m[:, 7], m[:, 8], ffn_w_in[i, 1], ffn_w_out[i, 1], ln_g[i, 2], ln_b[i, 2])
        if not last:
            xc = _post_ln(xc, mc[5] * yc, ln_g[i, 1], ln_b[i, 1])
            xc = _ffn_sublayer(xc, mc[6], mc[7], mc[8], ffn_w_in[i, 1], ffn_w_out[i, 1], ln_g[i, 2], ln_b[i, 2])
    return x
```

```python
import math
from contextlib import ExitStack
import numpy as np
import ml_dtypes
import concourse.bass as bass
import concourse.mybir as mybir
from concourse.bass_utils import run_bass_kernel_spmd

F32 = mybir.dt.float32
BF16 = mybir.dt.bfloat16
AF = mybir.ActivationFunctionType
ALU = mybir.AluOpType
AX = mybir.AxisListType

D = 1024
SEQ = 2048
CTX = 256
NT = SEQ + CTX
DFF = 2816
ALPHA = 4.0 ** 0.25
LN_EPS = 1e-6
EPS_P = LN_EPS / (ALPHA * ALPHA)
SB_BASE = 16512
SB_END = 212992
SB_CELL = 256
POOL_WIN = 48
PS_CELL = 2048
ENGS = ('pe', 'act', 'dve', 'pool', 'sp')
ISZ = {F32: 4, BF16: 2}


def _isz(dt):
    return ISZ[dt]


class Op:
    __slots__ = ('eng', 'f', 'deps', 'dmakey', 'ms', 'ord', 'seq', 'gidx', 'cost', 'fin', 'pos', 'fixed')


class Prog:
    def __init__(self, nc):
        self.nc = nc
        self.q = {e: [] for e in ENGS}
        self.W = {}
        self.R = {}
        self.tinfo = {}
        self.sb_off = SB_BASE
        self.dmacnt = {}
        self.dmalast = {}
        self.nops = 0
        self.fixed = False

    def sb(self, name, shape, dtype, off=None):
        nb = int(np.prod(shape[1:])) * _isz(dtype)
        if off is None:
            off = self.sb_off
            self.sb_off = (off + nb + 31) // 32 * 32
            assert self.sb_off <= SB_END, (name, self.sb_off)
        assert off % 32 == 0 and off + nb <= SB_END, (name, off, nb)
        h = self.nc.alloc_sbuf_tensor_at(name, list(shape), dtype, offset=off)
        self.tinfo[h.name] = ('sb', off)
        return h

    def ps_alloc(self):
        h = self.nc.alloc_psum_tensor("PSALL", [128, 8, 512], F32)
        self.tinfo[h.name] = ('ps', 0)
        return h

    def cells(self, ap):
        info = self.tinfo.get(ap.tensor.name)
        if info is None:
            return None
        space, base = info
        cell = SB_CELL if space == 'sb' else PS_CELL
        isz = _isz(ap.dtype)
        pat = ap.ap
        pstep = pat[0][0]
        off = ap.offset % pstep if pstep > 0 else ap.offset
        dims = [(s, c) for (s, c) in pat[1:] if c > 1]
        if not dims:
            dims = [(1, 1)]
        ins, inc = dims[-1]
        outer = dims[:-1]
        span = abs(ins) * (inc - 1) + 1
        lo_in = off + (ins * (inc - 1) if ins < 0 else 0)
        nouter = 1
        for s, c in outer:
            nouter *= c
        out = set()
        if nouter > 512:
            lo = lo_in + sum(min(0, s * (c - 1)) for s, c in outer)
            hi = lo_in + span + sum(max(0, s * (c - 1)) for s, c in outer)
            for cc in range((base + lo * isz) // cell, (base + hi * isz - 1) // cell + 1):
                out.add((space, cc))
            return out
        offs = [0]
        for s, c in outer:
            offs = [o + s * i for o in offs for i in range(c)]
        for o in offs:
            lo = (base + (lo_in + o) * isz)
            hi = lo + span * isz
            for cc in range(lo // cell, (hi - 1) // cell + 1):
                out.add((space, cc))
        return out

    def op(self, eng, f, outs=(), ins=(), rkeys=(), wkeys=(), dma=None):
        o = Op()
        o.eng = eng
        o.f = f
        o.dmakey = dma
        o.ms = False
        o.ord = 0
        o.seq = len(self.q[eng])
        o.gidx = self.nops
        self.nops += 1
        o.cost = self.est_cost(eng, outs, ins, dma)
        o.fin = 0.0
        o.pos = 0
        o.fixed = (self.fixed is True) or (isinstance(self.fixed, tuple) and eng in self.fixed)
        deps = {}
        rset = set(rkeys)
        wset = set(wkeys)
        for a in ins:
            c = self.cells(a)
            if c:
                rset |= c
        for a in outs:
            c = self.cells(a)
            if c:
                wset |= c

        def add(d):
            if d is None or d is o:
                return
            if d.dmakey is not None:
                deps[id(d)] = (d, self.dmacnt[d.dmakey])
            else:
                deps[id(d)] = (d, 0)
        for c in rset:
            add(self.W.get(c))
        for c in wset:
            add(self.W.get(c))
            for r in self.R.get(c, ()):
                add(r)
        for c in rset:
            if c not in wset:
                self.R.setdefault(c, []).append(o)
        for c in wset:
            self.W[c] = o
            self.R[c] = []
        o.deps = list(deps.values())
        if dma is not None:
            self.dmacnt[dma] = self.dmacnt.get(dma, 0) + 16
            self.dmalast[dma] = o
        self.q[eng].append(o)
        return o

    def est_cost(self, eng, outs, ins, dma):
        a = outs[0] if outs else (ins[0] if ins else None)
        if a is None:
            return 50.0
        F = 1
        for x in a.shape[1:]:
            F *= x
        if dma is not None:
            nb = F * a.shape[0] * _isz(a.dtype)
            return 2000.0 + nb / 200.0
        if eng == 'pe':
            c = 40.0 + 0.40 * F
            if ins and ins[0].dtype == F32:
                c *= 3.0
            return c
        if eng == 'act':
            return 100.0 + 1.0 * F
        if eng == 'dve':
            return 60.0 + 1.25 * F
        return 100.0 + 3.3 * F

    def schedule(self, window=128):
        LAT = 120.0
        rem = {e: list(self.q[e]) for e in ENGS}
        head = {e: 0 for e in ENGS}
        newq = {e: [] for e in ENGS}
        tfree = {e: 0.0 for e in ENGS}
        done = set()
        win = {'pe': window, 'act': window, 'dve': window, 'pool': POOL_WIN, 'sp': 1}
        total = sum(len(v) for v in rem.values())
        nsched = 0
        sched_flag = {}
        while nsched < total:
            best = None
            for e in ENGS:
                lst = rem[e]
                i = head[e]
                n = len(lst)
                cnt = 0
                be = None
                seen_dma = False
                while i < n and cnt < win[e]:
                    o = lst[i]
                    if o is not None:
                        cnt += 1
                        if o.fixed and cnt > 1:
                            break
                        if o.dmakey is not None:
                            if seen_dma:
                                i += 1
                                continue
                            seen_dma = True
                        ok = True
                        rdy = 0.0
                        for d, _ in o.deps:
                            if id(d) not in done:
                                ok = False
                                break
                            f = d.fin + (0.0 if d.eng == e else LAT)
                            if f > rdy:
                                rdy = f
                        if ok:
                            st = rdy if rdy > tfree[e] else tfree[e]
                            if be is None or st < be[0] - 1e-9:
                                be = (st, i, o)
                            if st <= tfree[e]:
                                break
                        if o.fixed:
                            break
                    i += 1
                if be is not None and (best is None or be[0] < best[0] - 1e-9):
                    best = (be[0], e, be[1], be[2])
            st, e, i, o = best
            issue = 100.0 if o.dmakey is not None else o.cost
            o.fin = st + o.cost
            tfree[e] = st + issue
            done.add(id(o))
            newq[e].append(o)
            rem[e][i] = None
            while head[e] < len(rem[e]) and rem[e][head[e]] is None:
                head[e] += 1
            nsched += 1
        self.q = newq
        self.sim_ns = max(tfree.values())

    def mm(self, out, lhsT, rhs, start=True, stop=True):
        return self.op('pe', lambda E: E.matmul(out, lhsT, rhs, start=start, stop=stop),
                       outs=[out], ins=[lhsT, rhs])

    def tr(self, out, in_, ident):
        return self.op('pe', lambda E: E.transpose(out, in_, ident), outs=[out], ins=[in_, ident])

    def act(self, out, in_, func, bias=0.0, scale=1.0, eng='act'):
        ins = [in_]
        if not isinstance(bias, (int, float)):
            ins.append(bias)
        if not isinstance(scale, (int, float)):
            ins.append(scale)
        return self.op('act', lambda E: E.activation(out, in_, func, bias=bias, scale=scale),
                       outs=[out], ins=ins)

    def tt(self, out, in0, in1, op, eng='dve'):
        return self.op(eng, lambda E: E.tensor_tensor(out, in0, in1, op), outs=[out], ins=[in0, in1])

    def ts(self, out, in0, s1, s2, op0, op1=None, eng='dve'):
        ins = [in0]
        if not isinstance(s1, (int, float)):
            ins.append(s1)
        if s2 is not None and not isinstance(s2, (int, float)):
            ins.append(s2)
        if op1 is None:
            return self.op(eng, lambda E: E.tensor_scalar(out, in0, s1, None, op0), outs=[out], ins=ins)
        return self.op(eng, lambda E: E.tensor_scalar(out, in0, s1, s2, op0, op1), outs=[out], ins=ins)

    def stt(self, out, in0, sc, in1, op0, op1, eng='dve'):
        ins = [in0, in1]
        if not isinstance(sc, (int, float)):
            ins.append(sc)
        return self.op(eng, lambda E: E.scalar_tensor_tensor(out, in0, sc, in1, op0, op1),
                       outs=[out], ins=ins)

    def copy(self, out, in_, eng='dve'):
        if eng == 'act':
            return self.op('act', lambda E: E.copy(out, in_), outs=[out], ins=[in_])
        return self.op(eng, lambda E: E.tensor_copy(out, in_), outs=[out], ins=[in_])

    def memset(self, out, val, eng='dve'):
        return self.op(eng, lambda E: E.memset(out, val), outs=[out])

    def dma(self, eng, out, in_, key, rkeys=(), wkeys=()):
        return self.op(eng, lambda E: E.dma_start(out=out, in_=in_), outs=[out], ins=[in_],
                       rkeys=rkeys, wkeys=wkeys, dma=key)

    def emit(self, final_keys):
        nc = self.nc
        fin = Op()
        fin.eng = 'sp'
        fin.f = None
        fin.dmakey = None
        fin.ms = False
        fin.ord = 0
        fin.deps = [(self.dmalast[k], self.dmacnt[k]) for k in final_keys]
        self.q['sp'].append(fin)
        for e in ENGS:
            for i, o in enumerate(self.q[e]):
                o.pos = i
        for e in ENGS:
            for o in self.q[e]:
                keep = {}
                for d, v in o.deps:
                    if d.dmakey is not None:
                        k = ('d', d.dmakey)
                        if k not in keep or keep[k][1] < v:
                            keep[k] = (d, v)
                    else:
                        if d.eng == 'pe' and o.eng == 'pe':
                            continue
                        k = ('e', d.eng)
                        if k not in keep or keep[k][0].pos < d.pos:
                            keep[k] = (d, 0)
                o.deps = list(keep.values())
                for d, _ in o.deps:
                    if d.dmakey is None:
                        d.ms = True
        for e in ENGS:
            cnt = 0
            for o in self.q[e]:
                if o.ms:
                    cnt += 1
                    o.ord = cnt
        nwaits = 0
        with ExitStack() as st:
            sems = {e: st.enter_context(nc.semaphore("s_" + e)) for e in ENGS}
            dsem = {k: st.enter_context(nc.semaphore("d_" + k)) for k in self.dmacnt}
            block = st.enter_context(nc.Block())
            secs = {'pe': block.tensor, 'act': block.scalar, 'dve': block.vector,
                    'pool': block.gpsimd, 'sp': block.sync}
            for e in ENGS:
                def body(E, e=e):
                    nonlocal nwaits
                    known = {}
                    for o in self.q[e]:
                        need = {}
                        for d, v in o.deps:
                            if d.dmakey is not None:
                                key = ('d', d.dmakey)
                                s = dsem[d.dmakey]
                            else:
                                if d.eng == 'pe' and e == 'pe':
                                    continue
                                key = ('e', d.eng)
                                s = sems[d.eng]
                                v = d.ord
                            if need.get(key, (None, 0))[1] < v:
                                need[key] = (s, v)
                        for key, (s, v) in need.items():
                            if known.get(key, 0) < v:
                                E.wait_ge(s, v)
                                known[key] = v
                                nwaits += 1
                        if o.f is None:
                            continue
                        ins = o.f(E)
                        if o.ms:
                            ins.then_inc(sems[e], 1)
                        if o.dmakey is not None:
                            ins.then_inc(dsem[o.dmakey], 16)
                secs[e](body)
        self.stats = {e: len(self.q[e]) for e in ENGS}
        self.stats['waits'] = nwaits


BLKS = [(0, 512), (512, 512), (1024, 512), (1536, 512), (2048, 256)]
FF_PARTS = [(0, 6), (6, 6), (12, 5), (17, 5)]
PI = math.pi
SCHED = True


def make_consts():
    bf = ml_dtypes.bfloat16
    f32 = np.float32
    C = {}
    C["identf"] = np.eye(128, dtype=f32)
    C["identb"] = np.eye(128).astype(bf)
    for L, nm in ((2048, "csL"), (256, "csC")):
        l = np.arange(L, dtype=np.int64)
        ang = 2.0 * np.pi * ((l[:, None] * l[None, :]) % L) / L
        cs = np.stack([np.cos(ang), np.sin(ang)], 0)
        nblk = L // 256
        nlc = L // 128
        t = cs.reshape(2, nlc, 128, nblk, 256).transpose(3, 2, 0, 1, 4)
        C[nm] = np.ascontiguousarray(t).astype(bf)
    c = np.arange(64, dtype=np.int64)
    a64 = 2.0 * np.pi * ((c[:, None] * c[None, :]) % 64) / 64
    bd = np.zeros((128, 2, 128))
    for h in range(2):
        bd[h * 64:(h + 1) * 64, 0, h * 64:(h + 1) * 64] = np.cos(a64)
        bd[h * 64:(h + 1) * 64, 1, h * 64:(h + 1) * 64] = -np.sin(a64)
    C["bd"] = bd.astype(bf)
    jj = np.arange(128)
    same = (jj[:, None] // 64) == (jj[None, :] // 64)
    tri = np.stack([same & (jj[:, None] <= jj[None, :]), same & (jj[:, None] >= jj[None, :])]).astype(f32)
    stri = np.stack([same & (jj[:, None] > jj[None, :]), same & (jj[:, None] < jj[None, :])]).astype(f32)
    C["tri"] = tri
    C["stri"] = stri
    C["tri1"] = np.stack([jj[:, None] <= jj[None, :], jj[:, None] >= jj[None, :]]).astype(f32)
    C["stri1"] = np.stack([jj[:, None] > jj[None, :], jj[:, None] < jj[None, :]]).astype(f32)
    for L, sfx in ((2048, "L"), (256, "C")):
        N = 2 * L
        t = np.arange(L, dtype=np.int64)
        f = np.arange(L, dtype=np.int64)
        ang = 2.0 * np.pi * ((t[:, None] * f[None, :]) % N) / N
        WF = np.zeros((L, N))
        WF[:, :L] = np.cos(ang)
        WF[:, L:] = -np.sin(ang)
        WF[:, L] = (-1.0) ** t
        nfc = N // 128
        ntc = L // 128
        C["wf" + sfx] = np.ascontiguousarray(WF.reshape(ntc, 128, nfc, 128).transpose(2, 1, 0, 3)).astype(bf)
        WI = np.zeros((N, L))
        WI[:L] = (2.0 / N) * np.cos(ang.T)
        WI[0] = 1.0 / N
        WI[L:] = -(2.0 / N) * np.sin(ang.T)
        WI[L] = (1.0 / N) * ((-1.0) ** t)
        bw = min(512, L)
        ntb = L // bw
        SL = 8 if L == 2048 else 4
        nslab = nfc // SL
        C["wi" + sfx] = np.ascontiguousarray(WI.reshape(nslab, SL, 128, ntb, bw).transpose(3, 0, 2, 1, 4)).astype(bf)
        tt = np.linspace(0.0, 1.0, L, dtype=f32)[:, None]
        fr = np.linspace(1e-4, 15, 16, dtype=f32)[None, :]
        idx = np.arange(L, dtype=f32)[:, None]
        w = (f32(2.0 * math.pi) * idx * fr) / f32(L)
        z = np.concatenate([tt, np.cos(w), -np.sin(w)], axis=-1).astype(f32)
        C["ze" + sfx] = np.ascontiguousarray(z.T)
        deltas = np.abs(np.linspace(math.log(1e-2) / 1.5, math.log(1e-2) / 0.3, 512, dtype=f32))
        dec = np.exp(-tt * deltas[None, :]).astype(f32)
        C["dec" + sfx] = np.ascontiguousarray(dec.reshape(L // 128, 128, 512).transpose(1, 0, 2))
    return C


_CONSTS = None


def get_consts():
    global _CONSTS
    if _CONSTS is None:
        _CONSTS = make_consts()
    return _CONSTS


NPDT = {np.dtype(np.float32): F32, np.dtype(ml_dtypes.bfloat16): BF16}


def run_threads(ths):
    ths = list(ths)
    while ths:
        for th in list(ths):
            try:
                next(th)
            except StopIteration:
                ths.remove(th)


class Region:
    def __init__(self, P, start):
        self.P = P
        self.off = start

    def a(self, name, shape, dt):
        nb = int(np.prod(shape[1:])) * _isz(dt)
        off = self.off
        self.off = (off + nb + 31) // 32 * 32
        Region.cnt = getattr(Region, 'cnt', 0) + 1
        return self.P.sb("%s_%d" % (name, Region.cnt), shape, dt, off=off)


def build(stop=None, nb=2):
    nc = bass.Bass("TRN2", target_bir_lowering=False)
    P = Prog(nc)

    def din(name, shape, dt=F32):
        return nc.dram_tensor(name, list(shape), dt, kind="ExternalInput")

    x_d = din("x", [nb, SEQ, D])
    ctx_d = din("ctx", [nb, CTX, D])
    c_d = din("c", [nb, D])
    cctx_d = din("c_ctx", [1, D])
    modw_d = din("mod_w", [2, D, 9 * D])
    modb_d = din("mod_b", [2, 9 * D])
    fwi_d = din("ffn_w_in", [2, 2, D, 2 * DFF])
    fwo_d = din("ffn_w_out", [2, 2, DFF, D])
    lng_d = din("ln_g", [2, 3, D])
    lnb_d = din("ln_b", [2, 3, D])
    evwi_d = din("ev_w_in", [D, 2048])
    evwo_d = din("ev_w_out", [D, D])
    hcw_d = din("hy_conv_w", [3, 1536])
    hcb_d = din("hy_conv_b", [1, 1536])
    hw1_d = din("hy_w1", [33, 64])
    hb1_d = din("hy_b1", [64, 1])
    hw2_d = din("hy_w2", [64, 64])
    hb2_d = din("hy_b2", [64, 1])
    hw3_d = din("hy_w3", [64, 2048])
    hfr_d = din("hy_freq", [64, 1])
    hsk_d = din("hy_skip", [2, 512])
    odwi_d = din("od_w_in", [D, 4128])
    odwo_d = din("od_w_out", [D, D])
    aup_d = din("gla_a_up", [2, 16, 256])
    abi_d = din("gla_a_b", [2, 256])
    gng_d = din("gla_norm_g", [128, 1])
    hlb_d = din("hg_lb", [2, 512])
    hng_d = din("hg_norm_g", [128, 1])
    cd = {}
    for k, v in get_consts().items():
        cd[k] = din(k, list(v.shape), NPDT[v.dtype])
    out_d = nc.dram_tensor("out", [nb, SEQ, D], F32, kind="ExternalOutput")
    kt_d = {"L": nc.dram_tensor("ktL", [2, 16, 128, 3, 512], F32),
            "C": nc.dram_tensor("ktC", [2, 2, 128, 3, 512], F32)}
    xg_d = nc.dram_tensor("xg", [2, 4, 128, NT], F32)
    ofm_d = [nc.dram_tensor("ofm%d" % i, [128, NT], BF16) for i in range(20)]
    offm_d = [nc.dram_tensor("offm%d" % i, [128, NT], BF16) for i in range(8)]
    otm_d = [nc.dram_tensor("otm%d" % i, [18, 128, 512], BF16) for i in range(3)]
    otf_d = [nc.dram_tensor("otf%d" % i, [18, 128, 512], F32) for i in range(2)]

    PS = P.ps_alloc()

    IDF = P.sb("IDF", [128, 128], F32)
    IDB = P.sb("IDB", [128, 128], BF16)
    ONES = P.sb("ONES", [128, 128], BF16)
    ONESF = P.sb("ONESF", [128, 128], F32)
    BD = P.sb("BD", [128, 2, 128], BF16)
    CT = P.sb("CT", [128, 8, 4], F32)
    SC = P.sb("SC", [128, 8, 4], BF16)
    MOD = [P.sb("MOD%d" % l, [128, 72, 3], F32) for l in range(2)]
    MB = P.sb("MB", [128, 2, 72], F32)
    LNG = P.sb("LNG", [128, 6, 8], F32)
    LNB = P.sb("LNB", [128, 6, 8], F32)
    CW = P.sb("CW", [128, 3, 12], F32)
    CB = P.sb("CB", [128, 12], F32)
    SKIP = P.sb("SKIP", [128, 2, 4], F32)
    EPSC = {}
    for ev in (LN_EPS, EPS_P):
        EPSC[ev] = P.sb('EPS%d' % len(EPSC), [128, 8], F32)
    r_set = P.sb_off
    X = P.sb("X", [128, 8, NT], F32)
    HT = P.sb("HT", [128, 8, NT], BF16)
    MEAN = P.sb("MEAN", [128, 512], F32)
    VAR = P.sb("VAR", [128, 512], F32)
    RSTD = P.sb("RSTD", [128, 512], F32)
    TT = [P.sb("TT%d" % i, [128, 512], F32) for i in range(2)]
    SS = [P.sb("SS%d" % i, [128, 512], BF16) for i in range(2)]
    rB = P.sb_off
    rH = P.tinfo[HT.name][1]
    r_afree = rH
    R = Region(P, rB)
    G = R.a("G", [128, 6, NT], BF16)
    XB = P.sb("XB", [128, 8, 512], BF16, off=rB)
    SQ = P.sb("SQ", [128, 8, 512], BF16, off=rB + 8192)
    STG = [P.sb("STG%d" % i, [128, D], F32, off=rB + 16384 + i * 4096) for i in range(2)]
    WO = R.a("WO", [128, 6, D], BF16)
    WIN = [R.a("WIN", [128, 2, 8, 384], BF16) for i in range(2)]
    WM = [P.sb("WM%d" % i, [128, 8, 512], BF16, off=P.tinfo[WIN[i].name][1]) for i in range(2)]
    assert R.off <= SB_END
    XBW = P.sb("XBW", [128, 8, 512], BF16, off=P.tinfo[WO.name][1])
    SQW = P.sb("SQW", [128, 8, 512], BF16, off=P.tinfo[WIN[1].name][1])
    XBQ = P.sb("XBQ", [128, 8, 512], BF16, off=P.tinfo[WIN[0].name][1])
    SQQ = P.sb("SQQ", [128, 8, 512], BF16, off=P.tinfo[WIN[1].name][1])

    def psb(bank, n=512):
        return PS[:, bank, 0:n]

    nc_ctx = nc.allow_non_contiguous_dma(reason="small strided param loads")
    nc_ctx.__enter__()

    P.dma('sp', IDF[:], cd["identf"].ap(), 'c0')
    P.dma('sp', IDB[:], cd["identb"].ap(), 'c0')
    P.dma('sp', BD[:], cd["bd"].ap(), 'c0')
    P.memset(ONES[:], 1.0 / 1024.0)
    P.memset(ONESF[:], 1.0)
    for ev in EPSC:
        P.memset(EPSC[ev][:], ev)
    for r in range(nb):
        P.dma('sp', CT[:, :, r], c_d[r].rearrange("(k p) -> p k", p=128), 'c0')
    P.dma('sp', CT[:, :, 2], cctx_d[0].rearrange("(k p) -> p k", p=128), 'c0')
    if nb < 2:
        P.dma('sp', CT[:, :, 1], cctx_d[0].rearrange("(k p) -> p k", p=128), 'c0')
    P.dma('sp', MB[:], modb_d.ap().rearrange("l (j p) -> p l j", p=128), 'c0')
    P.dma('sp', LNG[:], lng_d.ap().rearrange("l s (k p) -> p (l s) k", p=128), 'c0')
    P.dma('sp', LNB[:], lnb_d.ap().rearrange("l s (k p) -> p (l s) k", p=128), 'c0')
    P.dma('sp', CW[:], hcw_d.ap().rearrange("t (j p) -> p t j", p=128), 'c0')
    P.dma('sp', CB[:], hcb_d[0].rearrange("(j p) -> p j", p=128), 'c0')
    P.dma('sp', SKIP[:], hsk_d.ap().rearrange("o (j p) -> p o j", p=128), 'c0')
    P.act(SC[:, :, 0:3], CT[:, :, 0:3], AF.Silu)

    WMS = [P.sb("WMS%d" % i, [128, 8, 256], BF16, off=SB_END - 4096 * (i + 1)) for i in range(2)]

    def mod_gen():
        wslot = 0
        for l in range(2):
            pm = PS[:, 7, 0:216]
            for s_ in range(36):
                w = WMS[wslot % 2]
                P.dma('pool', w[:], modw_d[l, :, s_ * 256:(s_ + 1) * 256].rearrange("(k p) c -> p k c", p=128),
                      'wm%d' % (wslot % 2))
                wslot += 1
                for c4 in range(2):
                    j = s_ * 2 + c4
                    for k in range(8):
                        P.mm(pm[:, j * 3:(j + 1) * 3], w[:, k, c4 * 128:(c4 + 1) * 128], SC[:, k, 0:3],
                             start=(k == 0), stop=(k == 7))
                    yield
            M = MOD[l]
            for r in range(3):
                P.tt(M[:, :, r], PS[:, 7, 0:216].rearrange("p (j r) -> p j r", r=3)[:, :, r], MB[:, l, :], ALU.add)
            for v in (1, 4, 7):
                P.ts(M[:, v * 8:(v + 1) * 8, :], M[:, v * 8:(v + 1) * 8, :], 1.0, None, ALU.add)
            for v in (2, 8):
                P.ts(M[:, v * 8:(v + 1) * 8, :], M[:, v * 8:(v + 1) * 8, :], 0.5 / ALPHA, None, ALU.mult)
            P.ts(M[:, 40:48, :], M[:, 40:48, :], 1.0 / ALPHA, None, ALU.mult)
            yield

    def mcol(l, v, k, row):
        return MOD[l][:, v * 8 + k, row:row + 1]

    def hy_setup(sfx, L):
        nlc = L // 128
        bw = min(512, L)
        nbk = L // bw
        npair = L // 128
        S = Region(P, r_set)
        HF = [S.a("HF", [128, nlc, 512], F32) for _ in range(2)]
        SD = [[S.a("SD", [128, nlc, 512], BF16) for _ in range(2)] for _ in range(2)]
        s_mlp = S.off
        ZE = S.a("ZE", [128, L], F32)
        H1 = S.a("H1", [128, L], F32)
        H2 = S.a("H2", [128, L], F32)
        W1 = S.a("W1", [128, 64], F32)
        W2 = S.a("W2", [128, 64], F32)
        W3 = S.a("W3", [128, 2048], F32)
        PR = S.a("PR", [128, 8], F32)
        TA = S.a("TA", [128, 512], F32)
        TM_ = S.a("TM", [128, 512], F32)
        RN = S.a("RN", [128, 512], F32)
        DEC = [S.a("DEC", [128, 512], F32) for _ in range(2)]
        TAB = [S.a("TAB", [128, 512], BF16) for _ in range(2)]
        ONESB1 = S.a("ONESB1", [128, 128], BF16)
        P.memset(ONESB1[:], 1.0)
        S2 = Region(P, s_mlp)
        WFS = [S2.a("WFS", [128, 2, nlc, 128], BF16) for _ in range(2)]
        KTT = [S2.a("KTT", [128, 3, 512], F32) for _ in range(2)]
        assert S2.off <= SB_END - 8192 and S.off <= SB_END - 8192, (S.off, S2.off)
        assert S.off <= SB_END, S.off
        P.dma('sp', ZE[0:33, :], cd["ze" + sfx].ap(), 'hs')
        P.dma('sp', W1[0:33, :], hw1_d.ap(), 'hs')
        P.dma('sp', W2[0:64, :], hw2_d.ap(), 'hs')
        P.dma('sp', W3[0:64, :], hw3_d.ap(), 'hs')
        P.dma('sp', PR[0:64, 0:1], hfr_d.ap(), 'hs')
        P.dma('sp', PR[0:64, 1:2], hb1_d.ap(), 'hs')
        P.dma('sp', PR[0:64, 2:3], hb2_d.ap(), 'hs')
        P.tt(PR[0:64, 3:4], PR[0:64, 0:1], PR[0:64, 1:2], ALU.mult)
        P.tt(PR[0:64, 4:5], PR[0:64, 0:1], PR[0:64, 2:3], ALU.mult)

        def sinlayer(out, ps, fbcol):
            T = TA[0:64, 0:bw]
            M_ = TM_[0:64, 0:bw]
            P.ts(T, ps, PR[0:64, 0:1], PR[0:64, fbcol:fbcol + 1], ALU.mult, ALU.add)
            P.ts(M_, T, -PI, 2 * PI, ALU.is_lt, ALU.mult)
            P.tt(T, T, M_, ALU.add)
            P.ts(M_, T, PI, -2 * PI, ALU.is_gt, ALU.mult)
            P.tt(T, T, M_, ALU.add)
            P.act(out, T, AF.Sin)
        for bk in range(nbk):
            sl = slice(bk * bw, (bk + 1) * bw)
            P.mm(PS[0:64, 0, 0:bw], W1[0:33, :], ZE[0:33, sl])
            sinlayer(H1[0:64, sl], PS[0:64, 0, 0:bw], 3)
            yield
        for bk in range(nbk):
            sl = slice(bk * bw, (bk + 1) * bw)
            P.mm(PS[0:64, 0, 0:bw], W2[0:64, :], H1[0:64, sl])
            sinlayer(H2[0:64, sl], PS[0:64, 0, 0:bw], 4)
            yield
        dctr = 0
        for o in range(2):
            for lc in range(nlc):
                dec = DEC[dctr % 2]
                P.dma('sp', dec[:], cd["dec" + sfx][:, lc, :], 'dec%d' % (dctr % 2))
                dctr += 1
                for d in range(2):
                    ps = psb(1 + d)
                    P.mm(ps, H2[0:64, lc * 128:(lc + 1) * 128], W3[0:64, d * 1024 + o * 512:d * 1024 + (o + 1) * 512])
                    P.tt(HF[d][:, lc, :], ps, dec[:], ALU.mult)
                yield
            P.memset(HF[1][0:1, 0, :], 0.0)
            n_acc = 2 * nlc
            ia = 0
            for d in range(2):
                for lc in range(nlc):
                    tab = TAB[ia % 2]
                    P.stt(tab[:], HF[d][:, lc, :], -1.0, HF[d][:, lc, :], ALU.mult, ALU.max)
                    P.mm(psb(3), ONESB1[:], tab[:], start=(ia == 0), stop=(ia == n_acc - 1))
                    ia += 1
            P.ts(RN[:], psb(3), 1e-6, None, ALU.add)
            yield
            P.op('dve', lambda E: E.reciprocal(RN[:], RN[:]), outs=[RN[:]], ins=[RN[:]])
            for lc in range(nlc):
                P.tt(TA[:], HF[0][:, lc, :], HF[1][:, lc, :], ALU.add)
                P.tt(SD[o][0][:, lc, :], TA[:], RN[:], ALU.mult)
                P.tt(TM_[:], HF[0][:, lc, :], HF[1][:, lc, :], ALU.subtract, eng=('pool' if lc % 4 == 3 else 'dve'))
                P.tt(SD[o][1][:, lc, :], TM_[:], RN[:], ALU.mult, eng=('pool' if lc % 4 == 3 else 'dve'))
                yield
        kctr = 0
        for c in range(npair):
            wfs = WFS[c % 2]
            P.dma('sp', wfs[:, 0], cd["wf" + sfx][c], 'wfs%d' % (c % 2))
            P.dma('sp', wfs[:, 1], cd["wf" + sfx][npair + c], 'wfs%d' % (c % 2))
            for o in range(2):
                ktt = KTT[kctr % 2]
                kkey = 'ktw%d' % (kctr % 2)
                kctr += 1
                bp, bq = (4, 5) if kctr % 2 == 0 else (1, 2)
                for tc in range(nlc):
                    P.mm(psb(bp), wfs[:, 0, tc, :], SD[o][0][:, tc, :], start=(tc == 0), stop=(tc == nlc - 1))
                for tc in range(nlc):
                    P.mm(psb(bq), wfs[:, 1, tc, :], SD[o][1][:, tc, :], start=(tc == 0), stop=(tc == nlc - 1))
                P.copy(ktt[:, 0, :], psb(bp), eng='act')
                P.copy(ktt[:, 2, :], psb(bp), eng='act')
                P.copy(ktt[:, 1, :], psb(bq), eng='act')
                if c == 0:
                    for tc in range(nlc):
                        P.mm(psb(6), wfs[:, 1, tc, :], SD[o][0][:, tc, :], start=(tc == 0), stop=(tc == nlc - 1))
                    P.copy(ktt[0:1, 2, :], PS[0:1, 6, :], eng='dve')
                    P.memset(ktt[0:1, 1, :], 0.0)
                P.dma('sp', kt_d[sfx][o, c], ktt[:], kkey, wkeys=[('kt', sfx)])
                yield

    def ln_stats(t0, n, eps, XBv=None, SQv=None):
        XBv = XB if XBv is None else XBv
        SQv = SQ if SQv is None else SQv
        P.act(XBv[:, :, 0:n], X[:, :, t0:t0 + n], AF.Copy)
        P.act(SQv[:, :, 0:n], X[:, :, t0:t0 + n], AF.Square)
        for k in range(8):
            P.mm(psb(6, n), ONES[:], XBv[:, k, 0:n], start=(k == 0), stop=(k == 7))
        for k in range(8):
            P.mm(psb(7, n), ONES[:], SQv[:, k, 0:n], start=(k == 0), stop=(k == 7))
        P.copy(MEAN[:, 0:n], psb(6, n), eng='act')
        P.tt(VAR[:, 0:n], MEAN[:, 0:n], MEAN[:, 0:n], ALU.mult)
        P.tt(VAR[:, 0:n], psb(7, n), VAR[:, 0:n], ALU.subtract)
        P.act(RSTD[:, 0:n], VAR[:, 0:n], AF.Ln, bias=EPSC[eps][:, 0:1])
        P.act(RSTD[:, 0:n], RSTD[:, 0:n], AF.Exp, scale=-0.5)

    tctr = [0]

    def pre_ln_blk(l, v_shift, v_scale, row_b, bi, XBv=None, SQv=None):
        t0, n = BLKS[bi]
        row = row_b if bi < 4 else 2
        ln_stats(t0, n, LN_EPS, XBv, SQv)
        for k in range(8):
            T = TT[tctr[0] % 2]
            tctr[0] += 1
            P.tt(T[:, 0:n], X[:, k, t0:t0 + n], MEAN[:, 0:n], ALU.subtract)
            P.tt(T[:, 0:n], T[:, 0:n], RSTD[:, 0:n], ALU.mult, eng=('pool' if k % 3 == 2 else 'dve'))
            P.act(HT[:, k, t0:t0 + n], T[:, 0:n], AF.Identity,
                  bias=mcol(l, v_shift, k, row), scale=mcol(l, v_scale, k, row))

    def pre_ln(l, v_shift, v_scale, row_b, nblk):
        for bi in range(nblk):
            pre_ln_blk(l, v_shift, v_scale, row_b, bi)

    def post_ln_blk(l, s, bi, XBv=None, SQv=None):
        t0, n = BLKS[bi]
        ln_stats(t0, n, EPS_P, XBv, SQv)
        for k in range(8):
            T = TT[tctr[0] % 2]
            tctr[0] += 1
            P.tt(T[:, 0:n], X[:, k, t0:t0 + n], MEAN[:, 0:n], ALU.subtract)
            P.tt(T[:, 0:n], T[:, 0:n], RSTD[:, 0:n], ALU.mult, eng=('pool' if k % 3 == 2 else 'dve'))
            P.act(X[:, k, t0:t0 + n], T[:, 0:n], AF.Identity,
                  bias=LNB[:, l * 3 + s, k:k + 1], scale=LNG[:, l * 3 + s, k:k + 1])

    def post_ln(l, s, nblk):
        for bi in range(nblk):
            post_ln_blk(l, s, bi)

    wctr = [0]
    pctr = [0]

    def ffn(l, f, row_b, nblk):
        v0 = 0 if f == 0 else 6
        sidx = 0 if f == 0 else 2

        def load_w(c0, ns):
            w = WIN[wctr[0] % 2]
            key = 'wi%d' % (wctr[0] % 2)
            wctr[0] += 1
            P.dma('pool', w[:, 0, :, 0:ns * 128],
                  fwi_d[l, f, :, c0:c0 + ns * 128].rearrange("(k p) c -> p k c", p=128), key)
            P.dma('pool', w[:, 1, :, 0:ns * 128],
                  fwi_d[l, f, :, DFF + c0:DFF + c0 + ns * 128].rearrange("(k p) c -> p k c", p=128), key)
            return w

        def au(w, js, jq, bi):
            t0, n = BLKS[bi]
            pa = psb(pctr[0] % 2, n)
            pu = psb(2 + pctr[0] % 2, n)
            S_ = SS[pctr[0] % 2]
            pctr[0] += 1
            for k in range(8):
                P.mm(pa, w[:, 0, k, js * 128:(js + 1) * 128], HT[:, k, t0:t0 + n], start=(k == 0), stop=(k == 7))
            for k in range(8):
                P.mm(pu, w[:, 1, k, js * 128:(js + 1) * 128], HT[:, k, t0:t0 + n], start=(k == 0), stop=(k == 7))
            P.act(S_[:, 0:n], pa, AF.Silu)
            P.tt(G[:, jq, t0:t0 + n], pu, S_[:, 0:n], ALU.mult)

        def load_wo(j0, nq):
            P.dma('pool', WO[:, 0:nq, :],
                  fwo_d[l, f, j0 * 128:(j0 + nq) * 128, :].rearrange("(j p) c -> p j c", p=128), 'wo')

        for pi, (j0, nq) in enumerate(FF_PARTS):
            jj = 0
            if pi == 0:
                ns = min(3, nq)
                w = load_w(j0 * 128, ns)
                for bi in range(nblk):
                    pre_ln_blk(l, v0, v0 + 1, row_b, bi, XBW, SQW)
                    for js in range(ns):
                        au(w, js, js, bi)
                jj = ns
            load_wo(j0, nq)
            while jj < nq:
                ns = min(3, nq - jj)
                w = load_w((j0 + jj) * 128, ns)
                for js in range(ns):
                    for bi in range(nblk):
                        au(w, js, jj + js, bi)
                jj += ns
            last = (pi == len(FF_PARTS) - 1)
            for bi in range(nblk):
                t0, n = BLKS[bi]
                row = row_b if bi < 4 else 2
                for kf in range(8):
                    py = psb(4 + pctr[0] % 2, n)
                    pctr[0] += 1
                    for jq in range(nq):
                        P.mm(py, WO[:, jq, kf * 128:(kf + 1) * 128], G[:, jq, t0:t0 + n],
                             start=(jq == 0), stop=(jq == nq - 1))
                    P.stt(X[:, kf, t0:t0 + n], py, mcol(l, v0 + 2, kf, row), X[:, kf, t0:t0 + n],
                          ALU.mult, ALU.add)
                if last:
                    post_ln_blk(l, sidx, bi, XBQ, SQQ)

    def out_proj(l, row_b, nblk, w_d, r0, nch, Y, WOUTS):
        P.dma('pool', WOUTS[:, 0:nch, :], w_d[r0:r0 + nch * 128, :].rearrange("(j p) c -> p j c", p=128), 'wop')
        for bi in range(nblk):
            t0, n = BLKS[bi]
            row = row_b if bi < 4 else 2
            for kf in range(8):
                py = psb(6 + pctr[0] % 2, n)
                pctr[0] += 1
                for cc in range(nch):
                    P.mm(py, WOUTS[:, cc, kf * 128:(kf + 1) * 128], Y[:, cc, t0:t0 + n],
                         start=(cc == 0), stop=(cc == nch - 1))
                P.stt(X[:, kf, t0:t0 + n], py, mcol(l, 5, kf, row), X[:, kf, t0:t0 + n], ALU.mult, ALU.add)

    ectr = [0]

    def even_mixer(b, sub=None):
        pre_ln(0, 3, 4, b, 5)
        R2 = Region(P, rB)
        ZB = R2.a("ZB", [128, 4, NT], BF16)
        A_TM = R2.a("A_TM", [128, 18, 512], BF16)
        r_c = R2.off
        UB = R2.a("UB", [128, 2308], F32)
        XST = [R2.a("XST", [128, NT], F32) for _ in range(1)]
        WS = [R2.a("WS", [128, 8, 512], BF16), P.sb("WSb_%d" % b, [128, 8, 512], BF16, off=P.tinfo[MEAN.name][1])]
        assert R2.off <= SB_END, R2.off
        for ccol in (0, 2049, 2050, 2307):
            P.memset(UB[:, ccol:ccol + 1], 0.0)
        ws = WS[0]
        P.dma('pool', ws[:], evwi_d[:, 0:512].rearrange("(k p) c -> p k c", p=128), 'ews0')
        for t_ in range(18):
            ps = psb(t_ % 2)
            for k in range(8):
                P.mm(ps, HT[:, k, t_ * 128:(t_ + 1) * 128], ws[:, k, :], start=(k == 0), stop=(k == 7))
            P.copy(A_TM[:, t_, :], ps, eng=('act' if t_ % 2 == 0 else 'dve'))
        xs = 0
        for g in range(3):
            ws = WS[(g + 1) % 2]
            P.dma('pool', ws[:], evwi_d[:, 512 + g * 512:1024 + g * 512].rearrange("(k p) c -> p k c", p=128),
                  'ews%d' % ((g + 1) % 2))
            for cc in range(4):
                j = g * 4 + cc
                for bi in range(5):
                    t0, n = BLKS[bi]
                    ps = psb(2 + bi % 2, n)
                    for k in range(8):
                        P.mm(ps, ws[:, k, cc * 128:(cc + 1) * 128], HT[:, k, t0:t0 + n],
                             start=(k == 0), stop=(k == 7))
                    uc = t0 + 1 if bi < 4 else t0 + 3
                    P.copy(UB[:, uc:uc + n], ps, eng='act')
                xst = XST[0]
                for (uo, to, ln_) in ((1, 0, 2048), (2051, 2048, 256)):
                    acc = xst[:, to:to + ln_]
                    P.ts(acc, UB[:, uo:uo + ln_], CW[:, 1, j:j + 1], CB[:, j:j + 1], ALU.mult, ALU.add)
                    P.stt(acc, UB[:, uo - 1:uo - 1 + ln_], CW[:, 0, j:j + 1], acc, ALU.mult, ALU.add)
                    dst = ZB[:, cc, to:to + ln_] if g == 0 else acc
                    P.stt(dst, UB[:, uo + 1:uo + 1 + ln_], CW[:, 2, j:j + 1], acc, ALU.mult, ALU.add)
                if g > 0:
                    P.dma('sp', xg_d[g - 1, cc], xst[:], 'xst0', wkeys=[('xg', g - 1, cc)])
                    xs += 1
        if sub == 'mix0a':
            return
        R3 = Region(P, rH)
        YF = R3.a("YF", [128, 4, NT], BF16)
        T12 = [R3.a("T12", [128, 2, 256], BF16) for _ in range(2)]
        WOUTS = R3.a("WOUTS", [128, 4, D], BF16)
        CSS = [R3.a("CSS", [128, 2, 16, 256], BF16), P.sb("CSSb_%d" % b, [128, 2, 16, 256], BF16, off=r_c)]
        assert R3.off <= rB
        assert r_c + 16384 <= SB_END
        for (L, tile0, tok0, nm) in ((2048, 0, 0, "csL"), (256, 16, 2048, "csC")):
            nlc = L // 128
            scl = 1.0 / math.sqrt(L * 64.0)
            for bk in range(L // 256):
                css = CSS[ectr[0] % 2]
                P.dma('sp', css[:, :, 0:nlc, :], cd[nm][bk], 'css%d' % (ectr[0] % 2))
                for cc in range(4):
                    pb = 0 if (ectr[0] + cc) % 2 == 0 else 3
                    t12 = T12[(ectr[0] + cc) % 2]
                    for q in range(2):
                        for lc in range(nlc):
                            P.mm(psb(pb + q, 256), A_TM[:, tile0 + lc, cc * 128:(cc + 1) * 128], css[:, q, lc, :],
                                 start=(lc == 0), stop=(lc == nlc - 1))
                    P.copy(t12[:, 0, :], psb(pb, 256), eng='act')
                    P.copy(t12[:, 1, :], psb(pb + 1, 256), eng='dve')
                    P.mm(psb(pb + 2, 256), BD[:, 0, :], t12[:, 0, :], start=True, stop=False)
                    P.mm(psb(pb + 2, 256), BD[:, 1, :], t12[:, 1, :], start=False, stop=True)
                    P.act(YF[:, cc, tok0 + bk * 256:tok0 + (bk + 1) * 256], psb(pb + 2, 256), AF.Identity, scale=scl)
                ectr[0] += 1
        out_proj(0, b, 5, evwo_d, 0, 4, YF, WOUTS)
        if sub == 'mix0b':
            return
        Z_TM = P.sb("Z_TM_%d" % b, [128, 16, 512], BF16, off=rB + 18432)
        YH = P.sb("YH_%d" % b, [128, 32, 512], BF16, off=rB + 18432 + 16384)
        assert rB + 18432 + 16384 + 32768 <= SB_END
        R5 = Region(P, rH)
        WFS = [R5.a("WFS", [128, 2, 16, 128], BF16) for _ in range(2)]
        KTS = [R5.a("KTS", [128, 3, 512], F32) for _ in range(2)]
        TQ = [R5.a("TQ", [128, 512], F32) for _ in range(4)]
        WOUT2 = R5.a("WOUT2", [128, 4, D], BF16)
        XGS = [R5.a("XGS", [128, 512], F32) for _ in range(2)]
        assert R5.off <= rB, R5.off
        R6 = Region(P, rH)
        WIS = [R6.a("WIS", [128, 8, 512], BF16) for _ in range(2)]
        fctr = 0
        ictr = 0
        xctr = 0
        for (sfx, L, tok0) in (("L", 2048, 0), ("C", 256, 2048)):
            ntc = L // 128
            npair = L // 128
            bw = min(512, L)
            ntb = L // bw
            SL = 8 if L == 2048 else 4
            nslab = (2 * L // 128) // SL
            for o in range(2):
                for tc in range(ntc):
                    psT = PS[:, 4 + tc % 2, 0:256].bitcast(BF16)
                    for cc in range(4):
                        P.tr(psT[:, cc * 128:(cc + 1) * 128], ZB[:, cc, tok0 + tc * 128:tok0 + (tc + 1) * 128], IDB[:])
                    P.copy(Z_TM[:, tc, :], psT, eng=('act' if tc % 2 == 0 else 'dve'))
                for c in range(npair):
                    wfs = WFS[fctr % 2]
                    kts = KTS[fctr % 2]
                    P.dma('sp', wfs[:, 0, 0:ntc, :], cd["wf" + sfx][c], 'hwf%d' % (fctr % 2))
                    P.dma('sp', wfs[:, 1, 0:ntc, :], cd["wf" + sfx][npair + c], 'hwf%d' % (fctr % 2))
                    P.dma('sp', kts[:], kt_d[sfx][o, c], 'hkt%d' % (fctr % 2), rkeys=[('kt', sfx)])
                    pA = psb(0 + 2 * (fctr % 2))
                    pB = psb(1 + 2 * (fctr % 2))
                    fctr += 1
                    for tc in range(ntc):
                        P.mm(pA, wfs[:, 0, tc, :], Z_TM[:, tc, :], start=(tc == 0), stop=(tc == ntc - 1))
                    for tc in range(ntc):
                        P.mm(pB, wfs[:, 1, tc, :], Z_TM[:, tc, :], start=(tc == 0), stop=(tc == ntc - 1))
                    P.tt(TQ[0][:], pA, kts[:, 0, :], ALU.mult)
                    P.tt(TQ[1][:], pB, kts[:, 1, :], ALU.mult)
                    P.tt(YH[:, c, :], TQ[0][:], TQ[1][:], ALU.subtract, eng='pool')
                    P.tt(TQ[2][:], pA, kts[:, 1, :], ALU.mult)
                    P.tt(TQ[3][:], pB, kts[:, 2, :], ALU.mult)
                    P.tt(YH[:, npair + c, :], TQ[2][:], TQ[3][:], ALU.add, eng='pool')
                for tb in range(ntb):
                    for sl in range(nslab):
                        wis = WIS[ictr % 2]
                        P.dma('sp', wis[:, 0:SL, 0:bw], cd["wi" + sfx][tb, sl], 'hwi%d' % (ictr % 2))
                        ictr += 1
                        for cc in range(4):
                            for i in range(SL):
                                P.mm(psb(4 + cc, bw), YH[:, sl * SL + i, cc * 128:(cc + 1) * 128], wis[:, i, 0:bw],
                                     start=(sl == 0 and i == 0), stop=(sl == nslab - 1 and i == SL - 1))
                    for cc in range(4):
                        xgs = XGS[xctr % 2]
                        tq = TQ[xctr % 2]
                        tk = tok0 + tb * bw
                        P.dma('sp', xgs[:, 0:bw], xg_d[o, cc, :, tk:tk + bw], 'hxg%d' % (xctr % 2),
                              rkeys=[('xg', o, cc)])
                        xctr += 1
                        P.stt(tq[:, 0:bw], ZB[:, cc, tk:tk + bw], SKIP[:, o, cc:cc + 1], psb(4 + cc, bw),
                              ALU.mult, ALU.add)
                        P.tt(ZB[:, cc, tk:tk + bw], tq[:, 0:bw], xgs[:, 0:bw], ALU.mult, eng='pool')
        out_proj(0, b, 5, evwo_d, 512, 4, ZB, WOUT2)
        post_ln(0, 1, 5)

    def perm_blk(ap3, bi):
        return ap3.rearrange("p (r c) -> p c r", c=64)[:, 16 * bi:16 * bi + 16, :]

    def perm_tile(ap3, t_):
        return ap3.rearrange("p (r c) -> p c r", c=64)[:, 4 * t_:4 * t_ + 4, :]

    octr = [0]

    def odd_mixer(b):
        pre_ln(1, 3, 4, b, 5)
        RA = Region(P, rB)
        WS = [RA.a("OWS", [128, 8, 512], BF16) for _ in range(2)]
        SFM = [RA.a("SFM", [128, NT], BF16) for _ in range(2)]
        STMT = [RA.a("STMT", [128, 512], BF16) for _ in range(2)]
        STF = [RA.a("STF", [128, 512], F32) for _ in range(2)]
        HTP = RA.a("HTP", [128, 8, 2048], BF16)
        AFT = P.sb("AFT_%d" % b, [128, NT], F32, off=P.tinfo[MEAN.name][1])
        assert RA.off <= SB_END, RA.off
        for k in range(8):
            P.copy(HTP[:, k, :].rearrange("p (c r) -> p c r", r=32),
                   HT[:, k, 0:2048].rearrange("p (r c) -> p c r", c=64), eng=('act' if k % 2 == 0 else 'dve'))
        sctr = [0]
        fctr = [0]

        def load_slab(c0, ncol):
            w = WS[sctr[0] % 2]
            P.dma('pool', w[:, :, 0:ncol], odwi_d[:, c0:c0 + ncol].rearrange("(k p) c -> p k c", p=128),
                  'ows%d' % (sctr[0] % 2))
            sctr[0] += 1
            return w

        def fm_out(w, cs, M, perm, dst, f32, scale=1.0, prow=0, keep=None):
            if keep is None:
                st = SFM[fctr[0] % 2]
                skey = 'sfm%d' % (fctr[0] % 2)
                fctr[0] += 1
                stv = st[:]
            else:
                stv = keep
            for bi in range(5):
                t0, n = BLKS[bi]
                ps = PS[prow:prow + M, octr[0] % 4, 0:n]
                octr[0] += 1
                for k in range(8):
                    rhs = HTP[:, k, t0:t0 + n] if (perm and bi < 4) else HT[:, k, t0:t0 + n]
                    P.mm(ps, w[:, k, cs:cs + M], rhs, start=(k == 0), stop=(k == 7))
                P.act(stv[prow:prow + M, t0:t0 + n], ps, AF.Identity, scale=scale)
            if keep is None:
                P.dma('sp', dst[0:M, :], stv[0:M, :], skey, wkeys=[('of', id(dst))])

        def tm_out(w, cs, ncol, perm, dst, f32):
            for t_ in range(18):
                ps = PS[:, octr[0] % 4, 0:ncol]
                octr[0] += 1
                for k in range(8):
                    if t_ < 16:
                        lhs = HTP[:, k, t_ * 128:(t_ + 1) * 128] if perm else HT[:, k, t_ * 128:(t_ + 1) * 128]
                    else:
                        lhs = HT[:, k, t_ * 128:(t_ + 1) * 128]
                    P.mm(ps, lhs, w[:, k, cs:cs + ncol], start=(k == 0), stop=(k == 7))
                if f32:
                    st = STF[t_ % 2]
                    P.copy(st[:, 0:ncol], ps, eng=('act' if t_ % 2 == 0 else 'dve'))
                    P.dma('sp', dst[t_, :, 0:ncol], st[:, 0:ncol], 'stf%d' % (t_ % 2), wkeys=[('of', id(dst))])
                else:
                    st = STMT[t_ % 2]
                    P.copy(st[:, 0:ncol], ps, eng=('act' if t_ % 2 == 0 else 'dve'))
                    P.dma('sp', dst[t_, :, 0:ncol], st[:, 0:ncol], 'stm%d' % (t_ % 2), wkeys=[('of', id(dst))])

        w = load_slab(0, 512)
        for h in range(4):
            fm_out(w, h * 64, 64, False, ofm_d[h], False, scale=0.125)
        for h in range(4):
            fm_out(w, 256 + h * 64, 64, False, ofm_d[4 + h], False)
        tm_out(w, 256, 256, False, otm_d[0], False)
        w = load_slab(512, 512)
        tm_out(w, 0, 512, False, otm_d[1], False)
        w = load_slab(1024, 512)
        for h in range(4):
            fm_out(w, h * 128, 128, False, ofm_d[8 + h], False)
        w = load_slab(1536, 32)
        fm_out(w, 0, 16, False, None, True, prow=0, keep=AFT)
        fm_out(w, 16, 16, False, None, True, prow=32, keep=AFT)
        w = load_slab(1568, 512)
        for h in range(4):
            fm_out(w, h * 128, 128, True, ofm_d[12 + h], False)
        for d_ in range(2):
            w = load_slab(2080 + d_ * 512, 512)
            for h in range(4):
                fm_out(w, h * 128, 128, True, offm_d[d_ * 4 + h], False)
            tm_out(w, 0, 512, True, otf_d[d_], True)
        w = load_slab(3104, 512)
        tm_out(w, 0, 512, True, otm_d[2], False)
        w = load_slab(3616, 512)
        for h in range(4):
            fm_out(w, h * 128, 128, True, ofm_d[16 + h], False)

        RB_ = Region(P, r_afree)
        TRI = [RB_.a("TRI", [128, 128], F32) for i in range(2)]
        STR = [RB_.a("STR", [128, 128], F32) for i in range(2)]
        ONES8 = RB_.a("ONES8", [128, 128], BF16)
        ONE1 = RB_.a("ONE1", [128, 128], F32)
        ONEC = RB_.a("ONEC", [128, 8], F32)
        AUP = RB_.a("AUP", [128, 256], F32)
        ABI = RB_.a("ABI", [128, 256], F32)
        GNC = RB_.a("GNC", [128, 2], F32)
        LBB = RB_.a("LBB", [128, 512], F32)
        OMLB = RB_.a("OMLB", [128, 512], F32)
        LBC = RB_.a("LBC", [128, 4], F32)
        OMLC = RB_.a("OMLC", [128, 4], F32)
        LBT = RB_.a("LBT", [128, 2, 512], F32)
        LBT2 = RB_.a("LBT2", [128, 2, 4], F32)
        for i in range(2):
            P.dma('sp', TRI[i][:], cd["tri1"][i], 'oc')
            P.dma('sp', STR[i][:], cd["stri1"][i], 'oc')
            P.dma('sp', AUP[i * 32:i * 32 + 16, :], aup_d[i], 'oc')
            P.dma('sp', ABI[i * 32:i * 32 + 1, :], abi_d[i:i + 1, :], 'oc')
        P.memset(ONES8[:], 1.0 / 128.0)
        P.memset(ONE1[:], 1.0)
        P.memset(ONEC[:], 1.0)
        P.dma('sp', GNC[:, 0:1], gng_d.ap(), 'oc')
        P.dma('sp', GNC[:, 1:2], hng_d.ap(), 'oc')
        P.dma('sp', LBT[:], hlb_d.ap().partition_broadcast(128), 'oc')
        P.dma('sp', LBT2[:], hlb_d.ap().rearrange("l (j p) -> p l j", p=128), 'oc')
        P.tt(LBB[:], LBT[:, 1, :], LBT[:, 0, :], ALU.subtract)
        P.act(LBB[:], LBB[:], AF.Sigmoid)
        P.ts(OMLB[:], LBB[:], -1.0, 1.0, ALU.mult, ALU.add)
        P.tt(LBC[:], LBT2[:, 1, :], LBT2[:, 0, :], ALU.subtract)
        P.act(LBC[:], LBC[:], AF.Sigmoid)
        P.ts(OMLC[:], LBC[:], -1.0, 1.0, ALU.mult, ALU.add)
        QT = RB_.a("QT", [128, NT], BF16)
        KTb = RB_.a("KTb", [128, NT], BF16)
        XF = [RB_.a("XF", [128, NT], BF16) for _ in range(2)]
        XT = [RB_.a("XT", [128, 18, 128], F32) for _ in range(2)]
        KTM = RB_.a("KTM", [128, 18, 128], BF16)
        VTM = RB_.a("VTM", [128, 18, 128], BF16)
        GATE = RB_.a("GATE", [128, 2048], BF16)
        OT = RB_.a("OT", [128, 2048], F32)
        YU = RB_.a("YU", [128, 2048], BF16)
        WOU = RB_.a("WOU", [128, D], BF16)
        SQBS = [RB_.a("SQB", [128, 512], BF16) for _ in range(2)]
        RSS = [RB_.a("RS", [128, 512], F32) for _ in range(2)]
        SGS = [RB_.a("SG", [128, 512], F32) for _ in range(2)]
        TD = []
        for d_ in range(2):
            sets = []
            for q_ in range(3):
                sets.append(dict(
                    g=RB_.a("g", [128, 128], F32), eq=RB_.a("eq", [128, 128], F32), ek=RB_.a("ek", [128, 128], F32),
                    er=RB_.a("er", [128, 128], F32), ff=RB_.a("ff", [128, 128], F32), fk=RB_.a("fk", [128, 128], F32),
                    ft=RB_.a("ft", [128, 128], F32),
                    qe=RB_.a("qe", [128, 128], BF16), ke=RB_.a("ke", [128, 128], BF16), kh=RB_.a("kh", [128, 128], BF16),
                    at=RB_.a("at", [128, 128], BF16), dc=RB_.a("dc", [128, 8], F32)))
            TD.append(dict(sets=sets, S=RB_.a("S", [128, 128], F32), Sb=RB_.a("Sb", [128, 128], BF16)))
        assert RB_.off <= SB_END, RB_.off

        for m in range(2):
            dk = 64 if m == 0 else 128
            nchunk = 1 if m == 0 else 2
            cw = 128 // nchunk
            if m == 1:
                for i in range(2):
                    P.dma('sp', TRI[i][:], cd["tri"][i], 'oc')
                    P.dma('sp', STR[i][:], cd["stri"][i], 'oc')
            for h in range(4):
                unit = m * 4 + h
                if m == 0:
                    P.dma('sp', QT[0:64, :], ofm_d[h][0:64, :], 'ul', rkeys=[('of', id(ofm_d[h]))])
                    P.dma('sp', KTb[0:64, :], ofm_d[4 + h][0:64, :], 'ul', rkeys=[('of', id(ofm_d[4 + h]))])
                    P.dma('sp', KTM[:, :, 0:64], otm_d[0].ap().rearrange("t p c -> p t c")[:, :, h * 64:(h + 1) * 64],
                          'ul', rkeys=[('of', id(otm_d[0]))])
                    P.dma('sp', VTM[:], otm_d[1].ap().rearrange("t p c -> p t c")[:, :, h * 128:(h + 1) * 128],
                          'ul', rkeys=[('of', id(otm_d[1]))])
                    P.dma('sp', GATE[:], ofm_d[8 + h][:, 0:2048], 'ul', rkeys=[('of', id(ofm_d[8 + h]))])
                else:
                    P.dma('sp', QT[:], ofm_d[12 + h][:, :], 'ul', rkeys=[('of', id(ofm_d[12 + h]))])
                    for d_ in range(2):
                        P.dma('sp', XF[d_][:], offm_d[d_ * 4 + h][:, :], 'ul', rkeys=[('of', id(offm_d[d_ * 4 + h]))])
                        P.dma('sp', XT[d_][:], otf_d[d_].ap().rearrange("t p c -> p t c")[:, :, h * 128:(h + 1) * 128],
                              'ul', rkeys=[('of', id(otf_d[d_]))])
                    P.dma('sp', VTM[:], otm_d[2].ap().rearrange("t p c -> p t c")[:, :, h * 128:(h + 1) * 128],
                          'ul', rkeys=[('of', id(otm_d[2]))])
                    P.dma('sp', GATE[:], ofm_d[16 + h][:, 0:2048], 'ul', rkeys=[('of', id(ofm_d[16 + h]))])
                P.dma('pool', WOU[:], odwo_d[unit * 128:(unit + 1) * 128, :], 'uw')
                for d_ in range(2):
                    P.memset(TD[d_]['S'][:], 0.0)
                    P.memset(TD[d_]['Sb'][:], 0.0)
                touched = set()
                order = [[16, 17] + list(range(16)), [17, 16] + list(range(15, -1, -1))]

                def feat(s_, d_):
                    t_ = order[d_][s_]
                    T = TD[d_]['sets'][s_ % 3]
                    lat = t_ < 16
                    tk = slice(t_ * 128, (t_ + 1) * 128)
                    bk = 4 * d_
                    pGT = PS[0:dk, bk, 0:128]
                    pR = PS[:, bk + 1, 0:dk]
                    pX = PS[:, bk + 1, 256:256 + dk]
                    pAT = PS[:, bk + 2, 0:128]
                    g = T['g'][:, 0:dk]
                    if m == 0:
                        ar = 0 if d_ == 0 else 32
                        P.mm(pX, AFT[ar:ar + 16, tk], AUP[ar:ar + 16, h * 64:(h + 1) * 64], start=True, stop=False)
                        P.mm(pX, ONE1[ar:ar + 1, :], ABI[ar:ar + 1, h * 64:(h + 1) * 64], start=False, stop=True)
                        yield
                        P.act(T['ff'][:, 0:dk], pX, AF.Exp, scale=-1.0)
                        P.act(T['ff'][:, 0:dk], T['ff'][:, 0:dk], AF.Ln, bias=ONEC[:, 0:1])
                        yield
                        P.ts(g, T['ff'][:, 0:dk], -1.0 / 16.0, None, ALU.mult)
                        ktm = KTM[:, t_, 0:64]
                        ktf = KTb[0:64, tk]
                    else:
                        c0 = h * 128
                        P.act(T['ff'][:], XT[d_][:, t_, :], AF.Exp, scale=-1.0)
                        P.act(T['fk'][:], T['ff'][:], AF.Ln, bias=ONEC[:, 0:1])
                        if lat:
                            P.act(T['ft'][:], XF[d_][:, tk], AF.Exp, scale=-1.0)
                            P.act(T['ft'][:], T['ft'][:], AF.Ln, bias=ONEC[:, 0:1])
                        yield
                        P.tt(T['ff'][:], T['ff'][:], LBB[:, c0:c0 + 128], ALU.mult)
                        if lat:
                            P.tt(T['ft'][:], T['ft'][:], XF[d_][:, tk], ALU.add, eng='pool')
                        yield
                        P.act(T['ff'][:], T['ff'][:], AF.Ln, bias=ONEC[:, 0:1])
                        if lat:
                            P.act(T['ft'][:], T['ft'][:], AF.Exp, scale=-1.0)
                        yield
                        P.tt(g, T['ff'][:], T['fk'][:], ALU.subtract)
                        P.tt(T['fk'][:], T['fk'][:], XT[d_][:, t_, :], ALU.add)
                        if lat:
                            P.ts(T['ft'][:], T['ft'][:], OMLC[:, h:h + 1], None, ALU.mult)
                        yield
                        P.act(T['fk'][:], T['fk'][:], AF.Exp, scale=-1.0)
                        yield
                        P.tt(T['fk'][:], T['fk'][:], OMLB[:, c0:c0 + 128], ALU.mult)
                        ktm = T['fk'][:]
                        ktf = T['ft'][:]
                    yield
                    P.mm(pGT, g, TRI[d_][:], start=True, stop=True)
                    P.mm(pR, STR[d_][:], g, start=True, stop=True)
                    yield
                    P.act(T['er'][:, 0:dk], pR, AF.Exp)
                    if nchunk == 2:
                        ecol = [63, 127] if d_ == 0 else [0, 64]
                    else:
                        ecol = [127] if d_ == 0 else [0]
                    for ci in range(nchunk):
                        P.act(T['dc'][0:dk, ci:ci + 1], PS[0:dk, bk, ecol[ci]:ecol[ci] + 1], AF.Exp)
                    if lat:
                        P.act(T['eq'][0:dk, :], pGT, AF.Exp)
                        P.act(T['ek'][0:dk, :], pGT, AF.Exp, scale=-1.0)
                    yield
                    P.tt(T['kh'][:, 0:dk], ktm, T['er'][:, 0:dk], ALU.mult)
                    if lat:
                        P.tt(T['qe'][0:dk, :], QT[0:dk, tk], T['eq'][0:dk, :], ALU.mult)
                        P.tt(T['ke'][0:dk, :], ktf, T['ek'][0:dk, :], ALU.mult, eng='pool')
                        yield
                        P.mm(pAT, T['ke'][0:dk, :], T['qe'][0:dk, :], start=True, stop=True)
                        yield
                        P.tt(T['at'][:], pAT, TRI[d_][:], ALU.mult)

                def rec_chunk(s_, d_, ii):
                    t_ = order[d_][s_]
                    T = TD[d_]['sets'][s_ % 3]
                    St = TD[d_]
                    lat = t_ < 16
                    tk = slice(t_ * 128, (t_ + 1) * 128)
                    bk = 4 * d_
                    ci = ii if d_ == 0 else nchunk - 1 - ii
                    rows = slice(ci * cw, (ci + 1) * cw)
                    pB = PS[0:dk, bk + 2, 256 + 128 * ci:384 + 128 * ci]
                    pO = PS[:, bk + 3, 0:128]
                    if lat and ii == 0:
                        P.mm(pO, VTM[:, t_, :], T['at'][:], start=True, stop=False)
                    if lat:
                        P.mm(PS[:, bk + 3, ci * cw:(ci + 1) * cw], St['Sb'][0:dk, :], T['qe'][0:dk, rows],
                             start=False, stop=(ii == nchunk - 1))
                    P.mm(pB, T['kh'][rows, 0:dk], VTM[rows, t_, :], start=True, stop=True)
                    P.stt(St['Sb'][0:dk, :], St['S'][0:dk, :], T['dc'][0:dk, ci:ci + 1], pB, ALU.mult, ALU.add)
                    P.stt(St['S'][0:dk, :], St['S'][0:dk, :], T['dc'][0:dk, ci:ci + 1], pB, ALU.mult, ALU.add)
                    if lat and ii == nchunk - 1:
                        if t_ in touched:
                            P.tt(OT[:, tk], pO, OT[:, tk], ALU.add)
                        else:
                            P.copy(OT[:, tk], pO, eng='dve')
                            touched.add(t_)

                prog = {'feat': [0, 0], 'rec': 0}

                def featseq(d_):
                    for s_ in range(18):
                        while prog['rec'] < s_ - 2:
                            yield
                        yield from feat(s_, d_)
                        prog['feat'][d_] = s_ + 1
                        yield

                def rec_all():
                    for s_ in range(18):
                        while min(prog['feat']) <= s_:
                            yield
                        for ii in range(nchunk):
                            for d_ in range(2):
                                rec_chunk(s_, d_, ii)
                                yield
                        prog['rec'] = s_ + 1

                P.fixed = ('pe', 'act', 'pool', 'sp')
                run_threads([featseq(0), featseq(1), rec_all()])
                P.fixed = False
                def readout_blk(bi):
                    bs = slice(bi * 512, (bi + 1) * 512)
                    SGv, SQv, RSv = SGS[bi % 2], SQBS[bi % 2], RSS[bi % 2]
                    P.act(SGv[:], GATE[:, bs], AF.Exp, scale=-1.0)
                    yield
                    P.act(SGv[:], SGv[:], AF.Ln, bias=ONEC[:, 0:1])
                    yield
                    P.act(SGv[:], SGv[:], AF.Exp, scale=-1.0)
                    yield
                    if m == 1:
                        P.tt(OT[:, bs], OT[:, bs], SGv[:], ALU.mult)
                    else:
                        P.tt(SGv[:], SGv[:], GATE[:, bs], ALU.mult, eng='pool')
                    yield
                    P.act(SQv[:], OT[:, bs], AF.Square)
                    yield
                    P.mm(psb(6 + bi % 2), ONES8[:], SQv[:], start=True, stop=True)
                    yield
                    P.act(RSv[:], psb(6 + bi % 2), AF.Ln, bias=EPSC[LN_EPS][:, 0:1])
                    yield
                    P.act(RSv[:], RSv[:], AF.Exp, scale=-0.5)
                    yield
                    P.tt(RSv[:], OT[:, bs], RSv[:], ALU.mult)
                    yield
                    if m == 0:
                        P.stt(YU[:, bs], RSv[:], GNC[:, 0:1], SGv[:], ALU.mult, ALU.mult)
                    else:
                        P.ts(perm_blk(YU[:, 0:2048], bi), RSv[:].rearrange("p (c r) -> p c r", r=32), GNC[:, 1:2], None, ALU.mult)
                run_threads([readout_blk(0), readout_blk(1)])
                run_threads([readout_blk(2), readout_blk(3)])
                for bi in range(4):
                    t0, n = BLKS[bi]
                    rhs = YU[:, t0:t0 + n]
                    for kf in range(8):
                        py = psb(6 + pctr[0] % 2, n)
                        pctr[0] += 1
                        P.mm(py, WOU[:, kf * 128:(kf + 1) * 128], rhs, start=True, stop=True)
                        P.stt(X[:, kf, t0:t0 + n], py, mcol(1, 5, kf, b), X[:, kf, t0:t0 + n], ALU.mult, ALU.add)
        post_ln(1, 1, 4)

    lctr = [0]

    def load_x(b):
        for tt_ in range(18):
            stg = STG[lctr[0] % 2]
            key = 'ld%d' % (lctr[0] % 2)
            lctr[0] += 1
            src = x_d[b, tt_ * 128:(tt_ + 1) * 128, :] if tt_ < 16 else ctx_d[b, (tt_ - 16) * 128:(tt_ - 15) * 128, :]
            P.dma('sp', stg[:], src, key)
            for half in range(2):
                bank = 4 + half
                for kk in range(4):
                    k = half * 4 + kk
                    P.tr(PS[:, bank, kk * 128:(kk + 1) * 128], stg[:, k * 128:(k + 1) * 128], IDF[:])
                dst = X[:, half * 4:(half + 1) * 4, tt_ * 128:(tt_ + 1) * 128]
                srcp = PS[:, bank, :].rearrange("p (k t) -> p k t", k=4)
                P.copy(dst, srcp, eng=('act' if half == 0 else 'dve'))

    def store_x(b, ntile=16):
        for tt_ in range(ntile):
            stg = STG[lctr[0] % 2]
            key = 'st%d' % (lctr[0] % 2)
            lctr[0] += 1
            for half in range(2):
                bank = 4 + half
                for kk in range(4):
                    k = half * 4 + kk
                    P.tr(PS[:, bank, kk * 128:(kk + 1) * 128], X[:, k, tt_ * 128:(tt_ + 1) * 128], IDF[:])
                if half == 0:
                    P.copy(stg[:, 0:512], PS[:, bank, :], eng='act')
                else:
                    P.copy(stg[:, 512:1024], PS[:, bank, :], eng='dve')
            P.dma('sp', out_d[b, tt_ * 128:(tt_ + 1) * 128, :], stg[:], key)
        return ['st0', 'st1']

    stages = ['load', 'ffn1', 'mix0', 'ffn2', 'l1ffn1', 'mix1', None]
    sub = None
    if stop == 'ffnx2':
        sub = stop
        stop = 'ffn1'
    if stop in ('setup', 'mix0a', 'mix0b'):
        sub = stop
        stop = 'mix0'
    lim = stages.index(stop)
    def hy_chain():
        yield from hy_setup("L", 2048)
        yield from hy_setup("C", 256)
    if lim >= 2:
        run_threads([mod_gen(), hy_chain()])
    else:
        run_threads([mod_gen()])
    fkeys = []
    for b in range(nb):
        load_x(b)
        if lim >= 1:
            ffn(0, 0, b, 5)
        if sub == 'ffnx2':
            ffn(0, 1, b, 5)
        if lim >= 2 and sub != 'setup':
            even_mixer(b, sub)
        if lim >= 3:
            ffn(0, 1, b, 5)
        if lim >= 4:
            ffn(1, 0, b, 5)
        if lim >= 5:
            odd_mixer(b)
        if lim >= 6:
            ffn(1, 1, b, 4)
        fkeys = store_x(b)

    if SCHED:
        P.schedule()
    P.emit(fkeys)
    nc_ctx.__exit__(None, None, None)
    return nc, P


WEIGHT_KEYS = ["mod_w", "mod_b", "ffn_w_in", "ffn_w_out", "ln_g", "ln_b"]


def make_inputs(inputs, nb=2):
    f32 = np.float32
    ncores = 16 // nb

    def A(k, shape=None):
        a = np.ascontiguousarray(inputs[k], dtype=f32)
        return a.reshape(shape) if shape is not None else a
    common = {k: A(k) for k in WEIGHT_KEYS}
    common["c_ctx"] = A("c_ctx", (1, D))
    common["ev_w_in"] = A("ev_w_in", (D, 2048))
    common["ev_w_out"] = A("ev_w_out", (D, D))
    common["hy_conv_w"] = A("hy_conv_w", (3, 1536))
    common["hy_conv_b"] = A("hy_conv_b", (1, 1536))
    common["hy_w1"] = A("hy_w1", (33, 64))
    common["hy_b1"] = A("hy_b1", (64, 1))
    common["hy_w2"] = A("hy_w2", (64, 64))
    common["hy_b2"] = A("hy_b2", (64, 1))
    common["hy_w3"] = A("hy_w3", (64, 2048))
    common["hy_freq"] = A("hy_freq", (64, 1))
    common["hy_skip"] = A("hy_skip", (2, 512))
    common["od_w_in"] = A("od_w_in", (D, 4128))
    common["od_w_out"] = A("od_w_out", (D, D))
    common["gla_a_up"] = A("gla_a_up", (2, 16, 256))
    common["gla_a_b"] = A("gla_a_b", (2, 256))
    common["gla_norm_g"] = A("gla_norm_g", (128, 1))
    common["hg_lb"] = A("hg_lb", (2, 512))
    common["hg_norm_g"] = A("hg_norm_g", (128, 1))
    common.update(get_consts())
    maps = []
    for i in range(ncores):
        m = dict(common)
        m["x"] = np.ascontiguousarray(inputs["x"][i * nb:(i + 1) * nb], dtype=f32)
        m["ctx"] = np.ascontiguousarray(inputs["ctx"][i * nb:(i + 1) * nb], dtype=f32)
        m["c"] = np.ascontiguousarray(inputs["c"][i * nb:(i + 1) * nb], dtype=f32)
        maps.append(m)
    return maps


def kernel(**inputs):
    nc, P = build()
    maps = make_inputs(inputs)
    res = run_bass_kernel_spmd(nc, maps, core_ids=list(range(8)))
    return np.concatenate([np.asarray(r["out"], dtype=np.float32) for r in res.results], axis=0)
```

```python
import math
from contextlib import ExitStack
import numpy as np
import ml_dtypes
import concourse.bass as bass
import concourse.mybir as mybir
from concourse.bass_utils import run_bass_kernel_spmd

F32 = mybir.dt.float32
BF16 = mybir.dt.bfloat16
AF = mybir.ActivationFunctionType
ALU = mybir.AluOpType
AX = mybir.AxisListType

D = 1024
SEQ = 2048
CTX = 256
NT = SEQ + CTX
DFF = 2816
ALPHA = 4.0 ** 0.25
LN_EPS = 1e-6
EPS_P = LN_EPS / (ALPHA * ALPHA)
SB_BASE = 16512
SB_END = 212992
SB_CELL = 256
POOL_WIN = 48
PS_CELL = 2048
ENGS = ('pe', 'act', 'dve', 'pool', 'sp')
ISZ = {F32: 4, BF16: 2}


def _isz(dt):
    return ISZ[dt]


class Op:
    __slots__ = ('eng', 'f', 'deps', 'dmakey', 'ms', 'ord', 'seq', 'gidx', 'cost', 'fin', 'pos', 'fixed')


class Prog:
    def __init__(self, nc):
        self.nc = nc
        self.q = {e: [] for e in ENGS}
        self.W = {}
        self.R = {}
        self.tinfo = {}
        self.sb_off = SB_BASE
        self.dmacnt = {}
        self.dmalast = {}
        self.nops = 0
        self.fixed = False

    def sb(self, name, shape, dtype, off=None):
        nb = int(np.prod(shape[1:])) * _isz(dtype)
        if off is None:
            off = self.sb_off
            self.sb_off = (off + nb + 31) // 32 * 32
            assert self.sb_off <= SB_END, (name, self.sb_off)
        assert off % 32 == 0 and off + nb <= SB_END, (name, off, nb)
        h = self.nc.alloc_sbuf_tensor_at(name, list(shape), dtype, offset=off)
        self.tinfo[h.name] = ('sb', off)
        return h

    def ps_alloc(self):
        h = self.nc.alloc_psum_tensor("PSALL", [128, 8, 512], F32)
        self.tinfo[h.name] = ('ps', 0)
        return h

    def cells(self, ap):
        info = self.tinfo.get(ap.tensor.name)
        if info is None:
            return None
        space, base = info
        cell = SB_CELL if space == 'sb' else PS_CELL
        isz = _isz(ap.dtype)
        pat = ap.ap
        pstep = pat[0][0]
        off = ap.offset % pstep if pstep > 0 else ap.offset
        dims = [(s, c) for (s, c) in pat[1:] if c > 1]
        if not dims:
            dims = [(1, 1)]
        ins, inc = dims[-1]
        outer = dims[:-1]
        span = abs(ins) * (inc - 1) + 1
        lo_in = off + (ins * (inc - 1) if ins < 0 else 0)
        nouter = 1
        for s, c in outer:
            nouter *= c
        out = set()
        if nouter > 512:
            lo = lo_in + sum(min(0, s * (c - 1)) for s, c in outer)
            hi = lo_in + span + sum(max(0, s * (c - 1)) for s, c in outer)
            for cc in range((base + lo * isz) // cell, (base + hi * isz - 1) // cell + 1):
                out.add((space, cc))
            return out
        offs = [0]
        for s, c in outer:
            offs = [o + s * i for o in offs for i in range(c)]
        for o in offs:
            lo = (base + (lo_in + o) * isz)
            hi = lo + span * isz
            for cc in range(lo // cell, (hi - 1) // cell + 1):
                out.add((space, cc))
        return out

    def op(self, eng, f, outs=(), ins=(), rkeys=(), wkeys=(), dma=None):
        o = Op()
        o.eng = eng
        o.f = f
        o.dmakey = dma
        o.ms = False
        o.ord = 0
        o.seq = len(self.q[eng])
        o.gidx = self.nops
        self.nops += 1
        o.cost = self.est_cost(eng, outs, ins, dma)
        o.fin = 0.0
        o.pos = 0
        o.fixed = (self.fixed is True) or (isinstance(self.fixed, tuple) and eng in self.fixed)
        deps = {}
        rset = set(rkeys)
        wset = set(wkeys)
        for a in ins:
            c = self.cells(a)
            if c:
                rset |= c
        for a in outs:
            c = self.cells(a)
            if c:
                wset |= c

        def add(d):
            if d is None or d is o:
                return
            if d.dmakey is not None:
                deps[id(d)] = (d, self.dmacnt[d.dmakey])
            else:
                deps[id(d)] = (d, 0)
        for c in rset:
            add(self.W.get(c))
        for c in wset:
            add(self.W.get(c))
            for r in self.R.get(c, ()):
                add(r)
        for c in rset:
            if c not in wset:
                self.R.setdefault(c, []).append(o)
        for c in wset:
            self.W[c] = o
            self.R[c] = []
        o.deps = list(deps.values())
        if dma is not None:
            self.dmacnt[dma] = self.dmacnt.get(dma, 0) + 16
            self.dmalast[dma] = o
        self.q[eng].append(o)
        return o

    def est_cost(self, eng, outs, ins, dma):
        a = outs[0] if outs else (ins[0] if ins else None)
        if a is None:
            return 50.0
        F = 1
        for x in a.shape[1:]:
            F *= x
        if dma is not None:
            nb = F * a.shape[0] * _isz(a.dtype)
            return 2000.0 + nb / 200.0
        if eng == 'pe':
            c = 40.0 + 0.40 * F
            if ins and ins[0].dtype == F32:
                c *= 3.0
            return c
        if eng == 'act':
            return 100.0 + 1.0 * F
        if eng == 'dve':
            return 60.0 + 1.25 * F
        return 100.0 + 3.3 * F

    def schedule(self, window=128):
        LAT = 120.0
        rem = {e: list(self.q[e]) for e in ENGS}
        head = {e: 0 for e in ENGS}
        newq = {e: [] for e in ENGS}
        tfree = {e: 0.0 for e in ENGS}
        done = set()
        win = {'pe': window, 'act': window, 'dve': window, 'pool': POOL_WIN, 'sp': 1}
        total = sum(len(v) for v in rem.values())
        nsched = 0
        sched_flag = {}
        while nsched < total:
            best = None
            for e in ENGS:
                lst = rem[e]
                i = head[e]
                n = len(lst)
                cnt = 0
                be = None
                seen_dma = False
                while i < n and cnt < win[e]:
                    o = lst[i]
                    if o is not None:
                        cnt += 1
                        if o.fixed and cnt > 1:
                            break
                        if o.dmakey is not None:
                            if seen_dma:
                                i += 1
                                continue
                            seen_dma = True
                        ok = True
                        rdy = 0.0
                        for d, _ in o.deps:
                            if id(d) not in done:
                                ok = False
                                break
                            f = d.fin + (0.0 if d.eng == e else LAT)
                            if f > rdy:
                                rdy = f
                        if ok:
                            st = rdy if rdy > tfree[e] else tfree[e]
                            if be is None or st < be[0] - 1e-9:
                                be = (st, i, o)
                            if st <= tfree[e]:
                                break
                        if o.fixed:
                            break
                    i += 1
                if be is not None and (best is None or be[0] < best[0] - 1e-9):
                    best = (be[0], e, be[1], be[2])
            st, e, i, o = best
            issue = 100.0 if o.dmakey is not None else o.cost
            o.fin = st + o.cost
            tfree[e] = st + issue
            done.add(id(o))
            newq[e].append(o)
            rem[e][i] = None
            while head[e] < len(rem[e]) and rem[e][head[e]] is None:
                head[e] += 1
            nsched += 1
        self.q = newq
        self.sim_ns = max(tfree.values())

    def mm(self, out, lhsT, rhs, start=True, stop=True):
        return self.op('pe', lambda E: E.matmul(out, lhsT, rhs, start=start, stop=stop),
                       outs=[out], ins=[lhsT, rhs])

    def tr(self, out, in_, ident):
        return self.op('pe', lambda E: E.transpose(out, in_, ident), outs=[out], ins=[in_, ident])

    def act(self, out, in_, func, bias=0.0, scale=1.0, eng='act'):
        ins = [in_]
        if not isinstance(bias, (int, float)):
            ins.append(bias)
        if not isinstance(scale, (int, float)):
            ins.append(scale)
        return self.op('act', lambda E: E.activation(out, in_, func, bias=bias, scale=scale),
                       outs=[out], ins=ins)

    def tt(self, out, in0, in1, op, eng='dve'):
        return self.op(eng, lambda E: E.tensor_tensor(out, in0, in1, op), outs=[out], ins=[in0, in1])

    def ts(self, out, in0, s1, s2, op0, op1=None, eng='dve'):
        ins = [in0]
        if not isinstance(s1, (int, float)):
            ins.append(s1)
        if s2 is not None and not isinstance(s2, (int, float)):
            ins.append(s2)
        if op1 is None:
            return self.op(eng, lambda E: E.tensor_scalar(out, in0, s1, None, op0), outs=[out], ins=ins)
        return self.op(eng, lambda E: E.tensor_scalar(out, in0, s1, s2, op0, op1), outs=[out], ins=ins)

    def stt(self, out, in0, sc, in1, op0, op1, eng='dve'):
        ins = [in0, in1]
        if not isinstance(sc, (int, float)):
            ins.append(sc)
        return self.op(eng, lambda E: E.scalar_tensor_tensor(out, in0, sc, in1, op0, op1),
                       outs=[out], ins=ins)

    def copy(self, out, in_, eng='dve'):
        if eng == 'act':
            return self.op('act', lambda E: E.copy(out, in_), outs=[out], ins=[in_])
        return self.op(eng, lambda E: E.tensor_copy(out, in_), outs=[out], ins=[in_])

    def memset(self, out, val, eng='dve'):
        return self.op(eng, lambda E: E.memset(out, val), outs=[out])

    def dma(self, eng, out, in_, key, rkeys=(), wkeys=()):
        return self.op(eng, lambda E: E.dma_start(out=out, in_=in_), outs=[out], ins=[in_],
                       rkeys=rkeys, wkeys=wkeys, dma=key)

    def emit(self, final_keys):
        nc = self.nc
        fin = Op()
        fin.eng = 'sp'
        fin.f = None
        fin.dmakey = None
        fin.ms = False
        fin.ord = 0
        fin.deps = [(self.dmalast[k], self.dmacnt[k]) for k in final_keys]
        self.q['sp'].append(fin)
        for e in ENGS:
            for i, o in enumerate(self.q[e]):
                o.pos = i
        for e in ENGS:
            for o in self.q[e]:
                keep = {}
                for d, v in o.deps:
                    if d.dmakey is not None:
                        k = ('d', d.dmakey)
                        if k not in keep or keep[k][1] < v:
                            keep[k] = (d, v)
                    else:
                        if d.eng == 'pe' and o.eng == 'pe':
                            continue
                        k = ('e', d.eng)
                        if k not in keep or keep[k][0].pos < d.pos:
                            keep[k] = (d, 0)
                o.deps = list(keep.values())
                for d, _ in o.deps:
                    if d.dmakey is None:
                        d.ms = True
        for e in ENGS:
            cnt = 0
            for o in self.q[e]:
                if o.ms:
                    cnt += 1
                    o.ord = cnt
        nwaits = 0
        with ExitStack() as st:
            sems = {e: st.enter_context(nc.semaphore("s_" + e)) for e in ENGS}
            dsem = {k: st.enter_context(nc.semaphore("d_" + k)) for k in self.dmacnt}
            block = st.enter_context(nc.Block())
            secs = {'pe': block.tensor, 'act': block.scalar, 'dve': block.vector,
                    'pool': block.gpsimd, 'sp': block.sync}
            for e in ENGS:
                def body(E, e=e):
                    nonlocal nwaits
                    known = {}
                    for o in self.q[e]:
                        need = {}
                        for d, v in o.deps:
                            if d.dmakey is not None:
                                key = ('d', d.dmakey)
                                s = dsem[d.dmakey]
                            else:
                                if d.eng == 'pe' and e == 'pe':
                                    continue
                                key = ('e', d.eng)
                                s = sems[d.eng]
                                v = d.ord
                            if need.get(key, (None, 0))[1] < v:
                                need[key] = (s, v)
                        for key, (s, v) in need.items():
                            if known.get(key, 0) < v:
                                E.wait_ge(s, v)
                                known[key] = v
                                nwaits += 1
                        if o.f is None:
                            continue
                        ins = o.f(E)
                        if o.ms:
                            ins.then_inc(sems[e], 1)
                        if o.dmakey is not None:
                            ins.then_inc(dsem[o.dmakey], 16)
                secs[e](body)
        self.stats = {e: len(self.q[e]) for e in ENGS}
        self.stats['waits'] = nwaits


BLKS = [(0, 512), (512, 512), (1024, 512), (1536, 512), (2048, 256)]
FF_PARTS = [(0, 6), (6, 6), (12, 5), (17, 5)]
PI = math.pi
SCHED = True


def make_consts():
    bf = ml_dtypes.bfloat16
    f32 = np.float32
    C = {}
    C["identf"] = np.eye(128, dtype=f32)
    C["identb"] = np.eye(128).astype(bf)
    for L, nm in ((2048, "csL"), (256, "csC")):
        l = np.arange(L, dtype=np.int64)
        ang = 2.0 * np.pi * ((l[:, None] * l[None, :]) % L) / L
        cs = np.stack([np.cos(ang), np.sin(ang)], 0)
        nblk = L // 256
        nlc = L // 128
        t = cs.reshape(2, nlc, 128, nblk, 256).transpose(3, 2, 0, 1, 4)
        C[nm] = np.ascontiguousarray(t).astype(bf)
    c = np.arange(64, dtype=np.int64)
    a64 = 2.0 * np.pi * ((c[:, None] * c[None, :]) % 64) / 64
    bd = np.zeros((128, 2, 128))
    for h in range(2):
        bd[h * 64:(h + 1) * 64, 0, h * 64:(h + 1) * 64] = np.cos(a64)
        bd[h * 64:(h + 1) * 64, 1, h * 64:(h + 1) * 64] = -np.sin(a64)
    C["bd"] = bd.astype(bf)
    jj = np.arange(128)
    same = (jj[:, None] // 64) == (jj[None, :] // 64)
    tri = np.stack([same & (jj[:, None] <= jj[None, :]), same & (jj[:, None] >= jj[None, :])]).astype(f32)
    stri = np.stack([same & (jj[:, None] > jj[None, :]), same & (jj[:, None] < jj[None, :])]).astype(f32)
    C["tri"] = tri
    C["stri"] = stri
    C["tri1"] = np.stack([jj[:, None] <= jj[None, :], jj[:, None] >= jj[None, :]]).astype(f32)
    C["stri1"] = np.stack([jj[:, None] > jj[None, :], jj[:, None] < jj[None, :]]).astype(f32)
    for L, sfx in ((2048, "L"), (256, "C")):
        N = 2 * L
        t = np.arange(L, dtype=np.int64)
        f = np.arange(L, dtype=np.int64)
        ang = 2.0 * np.pi * ((t[:, None] * f[None, :]) % N) / N
        WF = np.zeros((L, N))
        WF[:, :L] = np.cos(ang)
        WF[:, L:] = -np.sin(ang)
        WF[:, L] = (-1.0) ** t
        nfc = N // 128
        ntc = L // 128
        C["wf" + sfx] = np.ascontiguousarray(WF.reshape(ntc, 128, nfc, 128).transpose(2, 1, 0, 3)).astype(bf)
        WI = np.zeros((N, L))
        WI[:L] = (2.0 / N) * np.cos(ang.T)
        WI[0] = 1.0 / N
        WI[L:] = -(2.0 / N) * np.sin(ang.T)
        WI[L] = (1.0 / N) * ((-1.0) ** t)
        bw = min(512, L)
        ntb = L // bw
        SL = 8 if L == 2048 else 4
        nslab = nfc // SL
        C["wi" + sfx] = np.ascontiguousarray(WI.reshape(nslab, SL, 128, ntb, bw).transpose(3, 0, 2, 1, 4)).astype(bf)
        tt = np.linspace(0.0, 1.0, L, dtype=f32)[:, None]
        fr = np.linspace(1e-4, 15, 16, dtype=f32)[None, :]
        idx = np.arange(L, dtype=f32)[:, None]
        w = (f32(2.0 * math.pi) * idx * fr) / f32(L)
        z = np.concatenate([tt, np.cos(w), -np.sin(w)], axis=-1).astype(f32)
        C["ze" + sfx] = np.ascontiguousarray(z.T)
        deltas = np.abs(np.linspace(math.log(1e-2) / 1.5, math.log(1e-2) / 0.3, 512, dtype=f32))
        dec = np.exp(-tt * deltas[None, :]).astype(f32)
        C["dec" + sfx] = np.ascontiguousarray(dec.reshape(L // 128, 128, 512).transpose(1, 0, 2))
    return C


_CONSTS = None


def get_consts():
    global _CONSTS
    if _CONSTS is None:
        _CONSTS = make_consts()
    return _CONSTS


NPDT = {np.dtype(np.float32): F32, np.dtype(ml_dtypes.bfloat16): BF16}


def run_threads(ths):
    ths = list(ths)
    while ths:
        for th in list(ths):
            try:
                next(th)
            except StopIteration:
                ths.remove(th)


class Region:
    def __init__(self, P, start):
        self.P = P
        self.off = start

    def a(self, name, shape, dt):
        nb = int(np.prod(shape[1:])) * _isz(dt)
        off = self.off
        self.off = (off + nb + 31) // 32 * 32
        Region.cnt = getattr(Region, 'cnt', 0) + 1
        return self.P.sb("%s_%d" % (name, Region.cnt), shape, dt, off=off)


def build(stop=None, nb=2):
    nc = bass.Bass("TRN2", target_bir_lowering=False)
    P = Prog(nc)

    def din(name, shape, dt=F32):
        return nc.dram_tensor(name, list(shape), dt, kind="ExternalInput")

    x_d = din("x", [nb, SEQ, D])
    ctx_d = din("ctx", [nb, CTX, D])
    c_d = din("c", [nb, D])
    cctx_d = din("c_ctx", [1, D])
    modw_d = din("mod_w", [2, D, 9 * D])
    modb_d = din("mod_b", [2, 9 * D])
    fwi_d = din("ffn_w_in", [2, 2, D, 2 * DFF])
    fwo_d = din("ffn_w_out", [2, 2, DFF, D])
    lng_d = din("ln_g", [2, 3, D])
    lnb_d = din("ln_b", [2, 3, D])
    evwi_d = din("ev_w_in", [D, 2048])
    evwo_d = din("ev_w_out", [D, D])
    hcw_d = din("hy_conv_w", [3, 1536])
    hcb_d = din("hy_conv_b", [1, 1536])
    hw1_d = din("hy_w1", [33, 64])
    hb1_d = din("hy_b1", [64, 1])
    hw2_d = din("hy_w2", [64, 64])
    hb2_d = din("hy_b2", [64, 1])
    hw3_d = din("hy_w3", [64, 2048])
    hfr_d = din("hy_freq", [64, 1])
    hsk_d = din("hy_skip", [2, 512])
    odwi_d = din("od_w_in", [D, 4128])
    odwo_d = din("od_w_out", [D, D])
    aup_d = din("gla_a_up", [2, 16, 256])
    abi_d = din("gla_a_b", [2, 256])
    gng_d = din("gla_norm_g", [128, 1])
    hlb_d = din("hg_lb", [2, 512])
    hng_d = din("hg_norm_g", [128, 1])
    cd = {}
    for k, v in get_consts().items():
        cd[k] = din(k, list(v.shape), NPDT[v.dtype])
    out_d = nc.dram_tensor("out", [nb, SEQ, D], F32, kind="ExternalOutput")
    kt_d = {"L": nc.dram_tensor("ktL", [2, 16, 128, 3, 512], F32),
            "C": nc.dram_tensor("ktC", [2, 2, 128, 3, 512], F32)}
    xg_d = nc.dram_tensor("xg", [2, 4, 128, NT], F32)
    ofm_d = [nc.dram_tensor("ofm%d" % i, [128, NT], BF16) for i in range(20)]
    offm_d = [nc.dram_tensor("offm%d" % i, [128, NT], BF16) for i in range(8)]
    otm_d = [nc.dram_tensor("otm%d" % i, [18, 128, 512], BF16) for i in range(3)]
    otf_d = [nc.dram_tensor("otf%d" % i, [18, 128, 512], F32) for i in range(2)]

    PS = P.ps_alloc()

    IDF = P.sb("IDF", [128, 128], F32)
    IDB = P.sb("IDB", [128, 128], BF16)
    ONES = P.sb("ONES", [128, 128], BF16)
    ONESF = P.sb("ONESF", [128, 128], F32)
    BD = P.sb("BD", [128, 2, 128], BF16)
    CT = P.sb("CT", [128, 8, 4], F32)
    SC = P.sb("SC", [128, 8, 4], BF16)
    MOD = [P.sb("MOD%d" % l, [128, 72, 3], F32) for l in range(2)]
    MB = P.sb("MB", [128, 2, 72], F32)
    LNG = P.sb("LNG", [128, 6, 8], F32)
    LNB = P.sb("LNB", [128, 6, 8], F32)
    CW = P.sb("CW", [128, 3, 12], F32)
    CB = P.sb("CB", [128, 12], F32)
    SKIP = P.sb("SKIP", [128, 2, 4], F32)
    EPSC = {}
    for ev in (LN_EPS, EPS_P):
        EPSC[ev] = P.sb('EPS%d' % len(EPSC), [128, 8], F32)
    r_set = P.sb_off
    X = P.sb("X", [128, 8, NT], F32)
    HT = P.sb("HT", [128, 8, NT], BF16)
    MEAN = P.sb("MEAN", [128, 512], F32)
    VAR = P.sb("VAR", [128, 512], F32)
    RSTD = P.sb("RSTD", [128, 512], F32)
    TT = [P.sb("TT%d" % i, [128, 512], F32) for i in range(2)]
    SS = [P.sb("SS%d" % i, [128, 512], BF16) for i in range(2)]
    rB = P.sb_off
    rH = P.tinfo[HT.name][1]
    r_afree = rH
    R = Region(P, rB)
    G = R.a("G", [128, 6, NT], BF16)
    XB = P.sb("XB", [128, 8, 512], BF16, off=rB)
    SQ = P.sb("SQ", [128, 8, 512], BF16, off=rB + 8192)
    STG = [P.sb("STG%d" % i, [128, D], F32, off=rB + 16384 + i * 4096) for i in range(2)]
    WO = R.a("WO", [128, 6, D], BF16)
    WIN = [R.a("WIN", [128, 2, 8, 384], BF16) for i in range(2)]
    WM = [P.sb("WM%d" % i, [128, 8, 512], BF16, off=P.tinfo[WIN[i].name][1]) for i in range(2)]
    assert R.off <= SB_END
    XBW = P.sb("XBW", [128, 8, 512], BF16, off=P.tinfo[WO.name][1])
    SQW = P.sb("SQW", [128, 8, 512], BF16, off=P.tinfo[WIN[1].name][1])
    XBQ = P.sb("XBQ", [128, 8, 512], BF16, off=P.tinfo[WIN[0].name][1])
    SQQ = P.sb("SQQ", [128, 8, 512], BF16, off=P.tinfo[WIN[1].name][1])

    def psb(bank, n=512):
        return PS[:, bank, 0:n]

    nc_ctx = nc.allow_non_contiguous_dma(reason="small strided param loads")
    nc_ctx.__enter__()

    P.dma('sp', IDF[:], cd["identf"].ap(), 'c0')
    P.dma('sp', IDB[:], cd["identb"].ap(), 'c0')
    P.dma('sp', BD[:], cd["bd"].ap(), 'c0')
    P.memset(ONES[:], 1.0 / 1024.0)
    P.memset(ONESF[:], 1.0)
    for ev in EPSC:
        P.memset(EPSC[ev][:], ev)
    for r in range(nb):
        P.dma('sp', CT[:, :, r], c_d[r].rearrange("(k p) -> p k", p=128), 'c0')
    P.dma('sp', CT[:, :, 2], cctx_d[0].rearrange("(k p) -> p k", p=128), 'c0')
    if nb < 2:
        P.dma('sp', CT[:, :, 1], cctx_d[0].rearrange("(k p) -> p k", p=128), 'c0')
    P.dma('sp', MB[:], modb_d.ap().rearrange("l (j p) -> p l j", p=128), 'c0')
    P.dma('sp', LNG[:], lng_d.ap().rearrange("l s (k p) -> p (l s) k", p=128), 'c0')
    P.dma('sp', LNB[:], lnb_d.ap().rearrange("l s (k p) -> p (l s) k", p=128), 'c0')
    P.dma('sp', CW[:], hcw_d.ap().rearrange("t (j p) -> p t j", p=128), 'c0')
    P.dma('sp', CB[:], hcb_d[0].rearrange("(j p) -> p j", p=128), 'c0')
    P.dma('sp', SKIP[:], hsk_d.ap().rearrange("o (j p) -> p o j", p=128), 'c0')
    P.act(SC[:, :, 0:3], CT[:, :, 0:3], AF.Silu)

    WMS = [P.sb("WMS%d" % i, [128, 8, 256], BF16, off=SB_END - 4096 * (i + 1)) for i in range(2)]

    def mod_gen():
        wslot = 0
        for l in range(2):
            pm = PS[:, 7, 0:216]
            for s_ in range(36):
                w = WMS[wslot % 2]
                P.dma('pool', w[:], modw_d[l, :, s_ * 256:(s_ + 1) * 256].rearrange("(k p) c -> p k c", p=128),
                      'wm%d' % (wslot % 2))
                wslot += 1
                for c4 in range(2):
                    j = s_ * 2 + c4
                    for k in range(8):
                        P.mm(pm[:, j * 3:(j + 1) * 3], w[:, k, c4 * 128:(c4 + 1) * 128], SC[:, k, 0:3],
                             start=(k == 0), stop=(k == 7))
                    yield
            M = MOD[l]
            for r in range(3):
                P.tt(M[:, :, r], PS[:, 7, 0:216].rearrange("p (j r) -> p j r", r=3)[:, :, r], MB[:, l, :], ALU.add)
            for v in (1, 4, 7):
                P.ts(M[:, v * 8:(v + 1) * 8, :], M[:, v * 8:(v + 1) * 8, :], 1.0, None, ALU.add)
            for v in (2, 8):
                P.ts(M[:, v * 8:(v + 1) * 8, :], M[:, v * 8:(v + 1) * 8, :], 0.5 / ALPHA, None, ALU.mult)
            P.ts(M[:, 40:48, :], M[:, 40:48, :], 1.0 / ALPHA, None, ALU.mult)
            yield

    def mcol(l, v, k, row):
        return MOD[l][:, v * 8 + k, row:row + 1]

    def hy_setup(sfx, L):
        nlc = L // 128
        bw = min(512, L)
        nbk = L // bw
        npair = L // 128
        S = Region(P, r_set)
        HF = [S.a("HF", [128, nlc, 512], F32) for _ in range(2)]
        SD = [[S.a("SD", [128, nlc, 512], BF16) for _ in range(2)] for _ in range(2)]
        s_mlp = S.off
        ZE = S.a("ZE", [128, L], F32)
        H1 = S.a("H1", [128, L], F32)
        H2 = S.a("H2", [128, L], F32)
        W1 = S.a("W1", [128, 64], F32)
        W2 = S.a("W2", [128, 64], F32)
        W3 = S.a("W3", [128, 2048], F32)
        PR = S.a("PR", [128, 8], F32)
        TA = S.a("TA", [128, 512], F32)
        TM_ = S.a("TM", [128, 512], F32)
        RN = S.a("RN", [128, 512], F32)
        DEC = [S.a("DEC", [128, 512], F32) for _ in range(2)]
        TAB = [S.a("TAB", [128, 512], BF16) for _ in range(2)]
        ONESB1 = S.a("ONESB1", [128, 128], BF16)
        P.memset(ONESB1[:], 1.0)
        S2 = Region(P, s_mlp)
        WFS = [S2.a("WFS", [128, 2, nlc, 128], BF16) for _ in range(2)]
        KTT = [S2.a("KTT", [128, 3, 512], F32) for _ in range(2)]
        assert S2.off <= SB_END - 8192 and S.off <= SB_END - 8192, (S.off, S2.off)
        assert S.off <= SB_END, S.off
        P.dma('sp', ZE[0:33, :], cd["ze" + sfx].ap(), 'hs')
        P.dma('sp', W1[0:33, :], hw1_d.ap(), 'hs')
        P.dma('sp', W2[0:64, :], hw2_d.ap(), 'hs')
        P.dma('sp', W3[0:64, :], hw3_d.ap(), 'hs')
        P.dma('sp', PR[0:64, 0:1], hfr_d.ap(), 'hs')
        P.dma('sp', PR[0:64, 1:2], hb1_d.ap(), 'hs')
        P.dma('sp', PR[0:64, 2:3], hb2_d.ap(), 'hs')
        P.tt(PR[0:64, 3:4], PR[0:64, 0:1], PR[0:64, 1:2], ALU.mult)
        P.tt(PR[0:64, 4:5], PR[0:64, 0:1], PR[0:64, 2:3], ALU.mult)

        def sinlayer(out, ps, fbcol):
            T = TA[0:64, 0:bw]
            M_ = TM_[0:64, 0:bw]
            P.ts(T, ps, PR[0:64, 0:1], PR[0:64, fbcol:fbcol + 1], ALU.mult, ALU.add)
            P.ts(M_, T, -PI, 2 * PI, ALU.is_lt, ALU.mult)
            P.tt(T, T, M_, ALU.add)
            P.ts(M_, T, PI, -2 * PI, ALU.is_gt, ALU.mult)
            P.tt(T, T, M_, ALU.add)
            P.act(out, T, AF.Sin)
        for bk in range(nbk):
            sl = slice(bk * bw, (bk + 1) * bw)
            P.mm(PS[0:64, 0, 0:bw], W1[0:33, :], ZE[0:33, sl])
            sinlayer(H1[0:64, sl], PS[0:64, 0, 0:bw], 3)
            yield
        for bk in range(nbk):
            sl = slice(bk * bw, (bk + 1) * bw)
            P.mm(PS[0:64, 0, 0:bw], W2[0:64, :], H1[0:64, sl])
            sinlayer(H2[0:64, sl], PS[0:64, 0, 0:bw], 4)
            yield
        dctr = 0
        for o in range(2):
            for lc in range(nlc):
                dec = DEC[dctr % 2]
                P.dma('sp', dec[:], cd["dec" + sfx][:, lc, :], 'dec%d' % (dctr % 2))
                dctr += 1
                for d in range(2):
                    ps = psb(1 + d)
                    P.mm(ps, H2[0:64, lc * 128:(lc + 1) * 128], W3[0:64, d * 1024 + o * 512:d * 1024 + (o + 1) * 512])
                    P.tt(HF[d][:, lc, :], ps, dec[:], ALU.mult)
                yield
            P.memset(HF[1][0:1, 0, :], 0.0)
            n_acc = 2 * nlc
            ia = 0
            for d in range(2):
                for lc in range(nlc):
                    tab = TAB[ia % 2]
                    P.stt(tab[:], HF[d][:, lc, :], -1.0, HF[d][:, lc, :], ALU.mult, ALU.max)
                    P.mm(psb(3), ONESB1[:], tab[:], start=(ia == 0), stop=(ia == n_acc - 1))
                    ia += 1
            P.ts(RN[:], psb(3), 1e-6, None, ALU.add)
            yield
            P.op('dve', lambda E: E.reciprocal(RN[:], RN[:]), outs=[RN[:]], ins=[RN[:]])
            for lc in range(nlc):
                P.tt(TA[:], HF[0][:, lc, :], HF[1][:, lc, :], ALU.add)
                P.tt(SD[o][0][:, lc, :], TA[:], RN[:], ALU.mult)
                P.tt(TM_[:], HF[0][:, lc, :], HF[1][:, lc, :], ALU.subtract, eng=('pool' if lc % 4 == 3 else 'dve'))
                P.tt(SD[o][1][:, lc, :], TM_[:], RN[:], ALU.mult, eng=('pool' if lc % 4 == 3 else 'dve'))
                yield
        kctr = 0
        for c in range(npair):
            wfs = WFS[c % 2]
            P.dma('sp', wfs[:, 0], cd["wf" + sfx][c], 'wfs%d' % (c % 2))
            P.dma('sp', wfs[:, 1], cd["wf" + sfx][npair + c], 'wfs%d' % (c % 2))
            for o in range(2):
                ktt = KTT[kctr % 2]
                kkey = 'ktw%d' % (kctr % 2)
                kctr += 1
                bp, bq = (4, 5) if kctr % 2 == 0 else (1, 2)
                for tc in range(nlc):
                    P.mm(psb(bp), wfs[:, 0, tc, :], SD[o][0][:, tc, :], start=(tc == 0), stop=(tc == nlc - 1))
                for tc in range(nlc):
                    P.mm(psb(bq), wfs[:, 1, tc, :], SD[o][1][:, tc, :], start=(tc == 0), stop=(tc == nlc - 1))
                P.copy(ktt[:, 0, :], psb(bp), eng='act')
                P.copy(ktt[:, 2, :], psb(bp), eng='act')
                P.copy(ktt[:, 1, :], psb(bq), eng='act')
                if c == 0:
                    for tc in range(nlc):
                        P.mm(psb(6), wfs[:, 1, tc, :], SD[o][0][:, tc, :], start=(tc == 0), stop=(tc == nlc - 1))
                    P.copy(ktt[0:1, 2, :], PS[0:1, 6, :], eng='dve')
                    P.memset(ktt[0:1, 1, :], 0.0)
                P.dma('sp', kt_d[sfx][o, c], ktt[:], kkey, wkeys=[('kt', sfx)])
                yield

    def ln_stats(t0, n, eps, XBv=None, SQv=None):
        XBv = XB if XBv is None else XBv
        SQv = SQ if SQv is None else SQv
        P.act(XBv[:, :, 0:n], X[:, :, t0:t0 + n], AF.Copy)
        P.act(SQv[:, :, 0:n], X[:, :, t0:t0 + n], AF.Square)
        for k in range(8):
            P.mm(psb(6, n), ONES[:], XBv[:, k, 0:n], start=(k == 0), stop=(k == 7))
        for k in range(8):
            P.mm(psb(7, n), ONES[:], SQv[:, k, 0:n], start=(k == 0), stop=(k == 7))
        P.copy(MEAN[:, 0:n], psb(6, n), eng='act')
        P.tt(VAR[:, 0:n], MEAN[:, 0:n], MEAN[:, 0:n], ALU.mult)
        P.tt(VAR[:, 0:n], psb(7, n), VAR[:, 0:n], ALU.subtract)
        P.act(RSTD[:, 0:n], VAR[:, 0:n], AF.Ln, bias=EPSC[eps][:, 0:1])
        P.act(RSTD[:, 0:n], RSTD[:, 0:n], AF.Exp, scale=-0.5)

    tctr = [0]

    def pre_ln_blk(l, v_shift, v_scale, row_b, bi, XBv=None, SQv=None):
        t0, n = BLKS[bi]
        row = row_b if bi < 4 else 2
        ln_stats(t0, n, LN_EPS, XBv, SQv)
        for k in range(8):
            T = TT[tctr[0] % 2]
            tctr[0] += 1
            P.tt(T[:, 0:n], X[:, k, t0:t0 + n], MEAN[:, 0:n], ALU.subtract)
            P.tt(T[:, 0:n], T[:, 0:n], RSTD[:, 0:n], ALU.mult, eng='dve')
            P.act(HT[:, k, t0:t0 + n], T[:, 0:n], AF.Identity,
                  bias=mcol(l, v_shift, k, row), scale=mcol(l, v_scale, k, row))

    def pre_ln(l, v_shift, v_scale, row_b, nblk):
        for bi in range(nblk):
            pre_ln_blk(l, v_shift, v_scale, row_b, bi)

    def post_ln_blk(l, s, bi, XBv=None, SQv=None):
        t0, n = BLKS[bi]
        ln_stats(t0, n, EPS_P, XBv, SQv)
        for k in range(8):
            T = TT[tctr[0] % 2]
            tctr[0] += 1
            P.tt(T[:, 0:n], X[:, k, t0:t0 + n], MEAN[:, 0:n], ALU.subtract)
            P.tt(T[:, 0:n], T[:, 0:n], RSTD[:, 0:n], ALU.mult, eng='dve')
            P.act(X[:, k, t0:t0 + n], T[:, 0:n], AF.Identity,
                  bias=LNB[:, l * 3 + s, k:k + 1], scale=LNG[:, l * 3 + s, k:k + 1])

    def post_ln(l, s, nblk):
        for bi in range(nblk):
            post_ln_blk(l, s, bi)

    wctr = [0]
    pctr = [0]

    def ffn(l, f, row_b, nblk):
        v0 = 0 if f == 0 else 6
        sidx = 0 if f == 0 else 2

        def load_w(c0, ns):
            w = WIN[wctr[0] % 2]
            key = 'wi%d' % (wctr[0] % 2)
            wctr[0] += 1
            P.dma('pool', w[:, 0, :, 0:ns * 128],
                  fwi_d[l, f, :, c0:c0 + ns * 128].rearrange("(k p) c -> p k c", p=128), key)
            P.dma('pool', w[:, 1, :, 0:ns * 128],
                  fwi_d[l, f, :, DFF + c0:DFF + c0 + ns * 128].rearrange("(k p) c -> p k c", p=128), key)
            return w

        def au(w, js, jq, bi):
            t0, n = BLKS[bi]
            pa = psb(pctr[0] % 2, n)
            pu = psb(2 + pctr[0] % 2, n)
            S_ = SS[pctr[0] % 2]
            pctr[0] += 1
            for k in range(8):
                P.mm(pa, w[:, 0, k, js * 128:(js + 1) * 128], HT[:, k, t0:t0 + n], start=(k == 0), stop=(k == 7))
            for k in range(8):
                P.mm(pu, w[:, 1, k, js * 128:(js + 1) * 128], HT[:, k, t0:t0 + n], start=(k == 0), stop=(k == 7))
            P.act(S_[:, 0:n], pa, AF.Silu)
            P.tt(G[:, jq, t0:t0 + n], pu, S_[:, 0:n], ALU.mult)

        def load_wo(j0, nq):
            P.dma('pool', WO[:, 0:nq, :],
                  fwo_d[l, f, j0 * 128:(j0 + nq) * 128, :].rearrange("(j p) c -> p j c", p=128), 'wo')

        for pi, (j0, nq) in enumerate(FF_PARTS):
            jj = 0
            if pi == 0:
                ns = min(3, nq)
                w = load_w(j0 * 128, ns)
                for bi in range(nblk):
                    pre_ln_blk(l, v0, v0 + 1, row_b, bi, XBW, SQW)
                    for js in range(ns):
                        au(w, js, js, bi)
                jj = ns
            load_wo(j0, nq)
            while jj < nq:
                ns = min(3, nq - jj)
                w = load_w((j0 + jj) * 128, ns)
                for js in range(ns):
                    for bi in range(nblk):
                        au(w, js, jj + js, bi)
                jj += ns
            last = (pi == len(FF_PARTS) - 1)
            for bi in range(nblk):
                t0, n = BLKS[bi]
                row = row_b if bi < 4 else 2
                for kf in range(8):
                    py = psb(4 + pctr[0] % 2, n)
                    pctr[0] += 1
                    for jq in range(nq):
                        P.mm(py, WO[:, jq, kf * 128:(kf + 1) * 128], G[:, jq, t0:t0 + n],
                             start=(jq == 0), stop=(jq == nq - 1))
                    P.stt(X[:, kf, t0:t0 + n], py, mcol(l, v0 + 2, kf, row), X[:, kf, t0:t0 + n],
                          ALU.mult, ALU.add)
                if last:
                    post_ln_blk(l, sidx, bi, XBQ, SQQ)

    def out_proj(l, row_b, nblk, w_d, r0, nch, Y, WOUTS):
        P.dma('pool', WOUTS[:, 0:nch, :], w_d[r0:r0 + nch * 128, :].rearrange("(j p) c -> p j c", p=128), 'wop')
        for bi in range(nblk):
            t0, n = BLKS[bi]
            row = row_b if bi < 4 else 2
            for kf in range(8):
                py = psb(6 + pctr[0] % 2, n)
                pctr[0] += 1
                for cc in range(nch):
                    P.mm(py, WOUTS[:, cc, kf * 128:(kf + 1) * 128], Y[:, cc, t0:t0 + n],
                         start=(cc == 0), stop=(cc == nch - 1))
                P.stt(X[:, kf, t0:t0 + n], py, mcol(l, 5, kf, row), X[:, kf, t0:t0 + n], ALU.mult, ALU.add)

    ectr = [0]

    def even_mixer(b, sub=None):
        pre_ln(0, 3, 4, b, 5)
        R2 = Region(P, rB)
        ZB = R2.a("ZB", [128, 4, NT], BF16)
        A_TM = R2.a("A_TM", [128, 18, 512], BF16)
        r_c = R2.off
        UB = R2.a("UB", [128, 2308], F32)
        XST = [R2.a("XST", [128, NT], F32) for _ in range(1)]
        WS = [R2.a("WS", [128, 8, 512], BF16), P.sb("WSb_%d" % b, [128, 8, 512], BF16, off=P.tinfo[MEAN.name][1])]
        assert R2.off <= SB_END, R2.off
        for ccol in (0, 2049, 2050, 2307):
            P.memset(UB[:, ccol:ccol + 1], 0.0)
        ws = WS[0]
        P.dma('pool', ws[:], evwi_d[:, 0:512].rearrange("(k p) c -> p k c", p=128), 'ews0')
        for t_ in range(18):
            ps = psb(t_ % 2)
            for k in range(8):
                P.mm(ps, HT[:, k, t_ * 128:(t_ + 1) * 128], ws[:, k, :], start=(k == 0), stop=(k == 7))
            P.copy(A_TM[:, t_, :], ps, eng=('act' if t_ % 2 == 0 else 'dve'))
        xs = 0
        for g in range(3):
            ws = WS[(g + 1) % 2]
            P.dma('pool', ws[:], evwi_d[:, 512 + g * 512:1024 + g * 512].rearrange("(k p) c -> p k c", p=128),
                  'ews%d' % ((g + 1) % 2))
            for cc in range(4):
                j = g * 4 + cc
                for bi in range(5):
                    t0, n = BLKS[bi]
                    ps = psb(2 + bi % 2, n)
                    for k in range(8):
                        P.mm(ps, ws[:, k, cc * 128:(cc + 1) * 128], HT[:, k, t0:t0 + n],
                             start=(k == 0), stop=(k == 7))
                    uc = t0 + 1 if bi < 4 else t0 + 3
                    P.copy(UB[:, uc:uc + n], ps, eng='act')
                xst = XST[0]
                for (uo, to, ln_) in ((1, 0, 2048), (2051, 2048, 256)):
                    acc = xst[:, to:to + ln_]
                    P.ts(acc, UB[:, uo:uo + ln_], CW[:, 1, j:j + 1], CB[:, j:j + 1], ALU.mult, ALU.add)
                    P.stt(acc, UB[:, uo - 1:uo - 1 + ln_], CW[:, 0, j:j + 1], acc, ALU.mult, ALU.add)
                    dst = ZB[:, cc, to:to + ln_] if g == 0 else acc
                    P.stt(dst, UB[:, uo + 1:uo + 1 + ln_], CW[:, 2, j:j + 1], acc, ALU.mult, ALU.add)
                if g > 0:
                    P.dma('sp', xg_d[g - 1, cc], xst[:], 'xst0', wkeys=[('xg', g - 1, cc)])
                    xs += 1
        if sub == 'mix0a':
            return
        R3 = Region(P, rH)
        YF = R3.a("YF", [128, 4, NT], BF16)
        T12 = [R3.a("T12", [128, 2, 256], BF16) for _ in range(2)]
        WOUTS = R3.a("WOUTS", [128, 4, D], BF16)
        CSS = [R3.a("CSS", [128, 2, 16, 256], BF16), P.sb("CSSb_%d" % b, [128, 2, 16, 256], BF16, off=r_c)]
        assert R3.off <= rB
        assert r_c + 16384 <= SB_END
        for (L, tile0, tok0, nm) in ((2048, 0, 0, "csL"), (256, 16, 2048, "csC")):
            nlc = L // 128
            scl = 1.0 / math.sqrt(L * 64.0)
            for bk in range(L // 256):
                css = CSS[ectr[0] % 2]
                P.dma('sp', css[:, :, 0:nlc, :], cd[nm][bk], 'css%d' % (ectr[0] % 2))
                for cc in range(4):
                    pb = 0 if (ectr[0] + cc) % 2 == 0 else 3
                    t12 = T12[(ectr[0] + cc) % 2]
                    for q in range(2):
                        for lc in range(nlc):
                            P.mm(psb(pb + q, 256), A_TM[:, tile0 + lc, cc * 128:(cc + 1) * 128], css[:, q, lc, :],
                                 start=(lc == 0), stop=(lc == nlc - 1))
                    P.copy(t12[:, 0, :], psb(pb, 256), eng='act')
                    P.copy(t12[:, 1, :], psb(pb + 1, 256), eng='dve')
                    P.mm(psb(pb + 2, 256), BD[:, 0, :], t12[:, 0, :], start=True, stop=False)
                    P.mm(psb(pb + 2, 256), BD[:, 1, :], t12[:, 1, :], start=False, stop=True)
                    P.act(YF[:, cc, tok0 + bk * 256:tok0 + (bk + 1) * 256], psb(pb + 2, 256), AF.Identity, scale=scl)
                ectr[0] += 1
        out_proj(0, b, 5, evwo_d, 0, 4, YF, WOUTS)
        if sub == 'mix0b':
            return
        Z_TM = P.sb("Z_TM_%d" % b, [128, 16, 512], BF16, off=rB + 18432)
        YH = P.sb("YH_%d" % b, [128, 32, 512], BF16, off=rB + 18432 + 16384)
        assert rB + 18432 + 16384 + 32768 <= SB_END
        R5 = Region(P, rH)
        WFS = [R5.a("WFS", [128, 2, 16, 128], BF16) for _ in range(2)]
        KTS = [R5.a("KTS", [128, 3, 512], F32) for _ in range(2)]
        TQ = [R5.a("TQ", [128, 512], F32) for _ in range(4)]
        WOUT2 = R5.a("WOUT2", [128, 4, D], BF16)
        XGS = [R5.a("XGS", [128, 512], F32) for _ in range(2)]
        assert R5.off <= rB, R5.off
        R6 = Region(P, rH)
        WIS = [R6.a("WIS", [128, 8, 512], BF16) for _ in range(2)]
        fctr = 0
        ictr = 0
        xctr = 0
        for (sfx, L, tok0) in (("L", 2048, 0), ("C", 256, 2048)):
            ntc = L // 128
            npair = L // 128
            bw = min(512, L)
            ntb = L // bw
            SL = 8 if L == 2048 else 4
            nslab = (2 * L // 128) // SL
            for o in range(2):
                for tc in range(ntc):
                    psT = PS[:, 4 + tc % 2, 0:256].bitcast(BF16)
                    for cc in range(4):
                        P.tr(psT[:, cc * 128:(cc + 1) * 128], ZB[:, cc, tok0 + tc * 128:tok0 + (tc + 1) * 128], IDB[:])
                    P.copy(Z_TM[:, tc, :], psT, eng=('act' if tc % 2 == 0 else 'dve'))
                for c in range(npair):
                    wfs = WFS[fctr % 2]
                    kts = KTS[fctr % 2]
                    P.dma('sp', wfs[:, 0, 0:ntc, :], cd["wf" + sfx][c], 'hwf%d' % (fctr % 2))
                    P.dma('sp', wfs[:, 1, 0:ntc, :], cd["wf" + sfx][npair + c], 'hwf%d' % (fctr % 2))
                    P.dma('sp', kts[:], kt_d[sfx][o, c], 'hkt%d' % (fctr % 2), rkeys=[('kt', sfx)])
                    pA = psb(0 + 2 * (fctr % 2))
                    pB = psb(1 + 2 * (fctr % 2))
                    fctr += 1
                    for tc in range(ntc):
                        P.mm(pA, wfs[:, 0, tc, :], Z_TM[:, tc, :], start=(tc == 0), stop=(tc == ntc - 1))
                    for tc in range(ntc):
                        P.mm(pB, wfs[:, 1, tc, :], Z_TM[:, tc, :], start=(tc == 0), stop=(tc == ntc - 1))
                    P.tt(TQ[0][:], pA, kts[:, 0, :], ALU.mult)
                    P.tt(TQ[1][:], pB, kts[:, 1, :], ALU.mult)
                    P.tt(YH[:, c, :], TQ[0][:], TQ[1][:], ALU.subtract, eng='pool')
                    P.tt(TQ[2][:], pA, kts[:, 1, :], ALU.mult)
                    P.tt(TQ[3][:], pB, kts[:, 2, :], ALU.mult)
                    P.tt(YH[:, npair + c, :], TQ[2][:], TQ[3][:], ALU.add, eng='pool')
                for tb in range(ntb):
                    for sl in range(nslab):
                        wis = WIS[ictr % 2]
                        P.dma('sp', wis[:, 0:SL, 0:bw], cd["wi" + sfx][tb, sl], 'hwi%d' % (ictr % 2))
                        ictr += 1
                        for cc in range(4):
                            for i in range(SL):
                                P.mm(psb(4 + cc, bw), YH[:, sl * SL + i, cc * 128:(cc + 1) * 128], wis[:, i, 0:bw],
                                     start=(sl == 0 and i == 0), stop=(sl == nslab - 1 and i == SL - 1))
                    for cc in range(4):
                        xgs = XGS[xctr % 2]
                        tq = TQ[xctr % 2]
                        tk = tok0 + tb * bw
                        P.dma('sp', xgs[:, 0:bw], xg_d[o, cc, :, tk:tk + bw], 'hxg%d' % (xctr % 2),
                              rkeys=[('xg', o, cc)])
                        xctr += 1
                        P.stt(tq[:, 0:bw], ZB[:, cc, tk:tk + bw], SKIP[:, o, cc:cc + 1], psb(4 + cc, bw),
                              ALU.mult, ALU.add)
                        P.tt(ZB[:, cc, tk:tk + bw], tq[:, 0:bw], xgs[:, 0:bw], ALU.mult, eng='dve')
        out_proj(0, b, 5, evwo_d, 512, 4, ZB, WOUT2)
        post_ln(0, 1, 5)

    def perm_blk(ap3, bi):
        return ap3.rearrange("p (r c) -> p c r", c=64)[:, 16 * bi:16 * bi + 16, :]

    def perm_tile(ap3, t_):
        return ap3.rearrange("p (r c) -> p c r", c=64)[:, 4 * t_:4 * t_ + 4, :]

    octr = [0]

    def odd_mixer(b):
        pre_ln(1, 3, 4, b, 5)
        RA = Region(P, rB)
        WS = [RA.a("OWS", [128, 8, 512], BF16) for _ in range(2)]
        SFM = [RA.a("SFM", [128, NT], BF16) for _ in range(2)]
        STMT = [RA.a("STMT", [128, 512], BF16) for _ in range(2)]
        STF = [RA.a("STF", [128, 512], F32) for _ in range(2)]
        HTP = RA.a("HTP", [128, 8, 2048], BF16)
        AFT = P.sb("AFT_%d" % b, [128, NT], F32, off=P.tinfo[MEAN.name][1])
        assert RA.off <= SB_END, RA.off
        for k in range(8):
            P.copy(HTP[:, k, :].rearrange("p (c r) -> p c r", r=32),
                   HT[:, k, 0:2048].rearrange("p (r c) -> p c r", c=64), eng=('act' if k % 2 == 0 else 'dve'))
        sctr = [0]
        fctr = [0]

        def load_slab(c0, ncol):
            w = WS[sctr[0] % 2]
            P.dma('pool', w[:, :, 0:ncol], odwi_d[:, c0:c0 + ncol].rearrange("(k p) c -> p k c", p=128),
                  'ows%d' % (sctr[0] % 2))
            sctr[0] += 1
            return w

        def fm_out(w, cs, M, perm, dst, f32, scale=1.0, prow=0, keep=None):
            if keep is None:
                st = SFM[fctr[0] % 2]
                skey = 'sfm%d' % (fctr[0] % 2)
                fctr[0] += 1
                stv = st[:]
            else:
                stv = keep
            for bi in range(5):
                t0, n = BLKS[bi]
                ps = PS[prow:prow + M, octr[0] % 4, 0:n]
                octr[0] += 1
                for k in range(8):
                    rhs = HTP[:, k, t0:t0 + n] if (perm and bi < 4) else HT[:, k, t0:t0 + n]
                    P.mm(ps, w[:, k, cs:cs + M], rhs, start=(k == 0), stop=(k == 7))
                P.act(stv[prow:prow + M, t0:t0 + n], ps, AF.Identity, scale=scale)
            if keep is None:
                P.dma('sp', dst[0:M, :], stv[0:M, :], skey, wkeys=[('of', id(dst))])

        def tm_out(w, cs, ncol, perm, dst, f32):
            for t_ in range(18):
                ps = PS[:, octr[0] % 4, 0:ncol]
                octr[0] += 1
                for k in range(8):
                    if t_ < 16:
                        lhs = HTP[:, k, t_ * 128:(t_ + 1) * 128] if perm else HT[:, k, t_ * 128:(t_ + 1) * 128]
                    else:
                        lhs = HT[:, k, t_ * 128:(t_ + 1) * 128]
                    P.mm(ps, lhs, w[:, k, cs:cs + ncol], start=(k == 0), stop=(k == 7))
                if f32:
                    st = STF[t_ % 2]
                    P.copy(st[:, 0:ncol], ps, eng=('act' if t_ % 2 == 0 else 'dve'))
                    P.dma('sp', dst[t_, :, 0:ncol], st[:, 0:ncol], 'stf%d' % (t_ % 2), wkeys=[('of', id(dst))])
                else:
                    st = STMT[t_ % 2]
                    P.copy(st[:, 0:ncol], ps, eng=('act' if t_ % 2 == 0 else 'dve'))
                    P.dma('sp', dst[t_, :, 0:ncol], st[:, 0:ncol], 'stm%d' % (t_ % 2), wkeys=[('of', id(dst))])

        w = load_slab(0, 512)
        for h in range(4):
            fm_out(w, h * 64, 64, False, ofm_d[h], False, scale=0.125)
        for h in range(4):
            fm_out(w, 256 + h * 64, 64, False, ofm_d[4 + h], False)
        tm_out(w, 256, 256, False, otm_d[0], False)
        w = load_slab(512, 512)
        tm_out(w, 0, 512, False, otm_d[1], False)
        w = load_slab(1024, 512)
        for h in range(4):
            fm_out(w, h * 128, 128, False, ofm_d[8 + h], False)
        w = load_slab(1536, 32)
        fm_out(w, 0, 16, False, None, True, prow=0, keep=AFT)
        fm_out(w, 16, 16, False, None, True, prow=32, keep=AFT)
        w = load_slab(1568, 512)
        for h in range(4):
            fm_out(w, h * 128, 128, True, ofm_d[12 + h], False)
        for d_ in range(2):
            w = load_slab(2080 + d_ * 512, 512)
            for h in range(4):
                fm_out(w, h * 128, 128, True, offm_d[d_ * 4 + h], False)
            tm_out(w, 0, 512, True, otf_d[d_], True)
        w = load_slab(3104, 512)
        tm_out(w, 0, 512, True, otm_d[2], False)
        w = load_slab(3616, 512)
        for h in range(4):
            fm_out(w, h * 128, 128, True, ofm_d[16 + h], False)

        RB_ = Region(P, r_afree)
        TRI = [RB_.a("TRI", [128, 128], F32) for i in range(2)]
        STR = [RB_.a("STR", [128, 128], F32) for i in range(2)]
        ONES8 = RB_.a("ONES8", [128, 128], BF16)
        ONE1 = RB_.a("ONE1", [128, 128], F32)
        ONEC = RB_.a("ONEC", [128, 8], F32)
        AUP = RB_.a("AUP", [128, 256], F32)
        ABI = RB_.a("ABI", [128, 256], F32)
        GNC = RB_.a("GNC", [128, 2], F32)
        LBB = RB_.a("LBB", [128, 512], F32)
        OMLB = RB_.a("OMLB", [128, 512], F32)
        LBC = RB_.a("LBC", [128, 4], F32)
        OMLC = RB_.a("OMLC", [128, 4], F32)
        LBT = RB_.a("LBT", [128, 2, 512], F32)
        LBT2 = RB_.a("LBT2", [128, 2, 4], F32)
        for i in range(2):
            P.dma('sp', TRI[i][:], cd["tri1"][i], 'oc')
            P.dma('sp', STR[i][:], cd["stri1"][i], 'oc')
            P.dma('sp', AUP[i * 32:i * 32 + 16, :], aup_d[i], 'oc')
            P.dma('sp', ABI[i * 32:i * 32 + 1, :], abi_d[i:i + 1, :], 'oc')
        P.memset(ONES8[:], 1.0 / 128.0)
        P.memset(ONE1[:], 1.0)
        P.memset(ONEC[:], 1.0)
        P.dma('sp', GNC[:, 0:1], gng_d.ap(), 'oc')
        P.dma('sp', GNC[:, 1:2], hng_d.ap(), 'oc')
        P.dma('sp', LBT[:], hlb_d.ap().partition_broadcast(128), 'oc')
        P.dma('sp', LBT2[:], hlb_d.ap().rearrange("l (j p) -> p l j", p=128), 'oc')
        P.tt(LBB[:], LBT[:, 1, :], LBT[:, 0, :], ALU.subtract)
        P.act(LBB[:], LBB[:], AF.Sigmoid)
        P.ts(OMLB[:], LBB[:], -1.0, 1.0, ALU.mult, ALU.add)
        P.tt(LBC[:], LBT2[:, 1, :], LBT2[:, 0, :], ALU.subtract)
        P.act(LBC[:], LBC[:], AF.Sigmoid)
        P.ts(OMLC[:], LBC[:], -1.0, 1.0, ALU.mult, ALU.add)
        QT = RB_.a("QT", [128, NT], BF16)
        KTb = RB_.a("KTb", [128, NT], BF16)
        XF = [RB_.a("XF", [128, NT], BF16) for _ in range(2)]
        XT = [RB_.a("XT", [128, 18, 128], F32) for _ in range(2)]
        KTM = RB_.a("KTM", [128, 18, 128], BF16)
        VTM = RB_.a("VTM", [128, 18, 128], BF16)
        GATE = RB_.a("GATE", [128, 2048], BF16)
        OT = RB_.a("OT", [128, 2048], F32)
        YU = RB_.a("YU", [128, 2048], BF16)
        WOU = RB_.a("WOU", [128, D], BF16)
        SQBS = [RB_.a("SQB", [128, 512], BF16) for _ in range(2)]
        RSS = [RB_.a("RS", [128, 512], F32) for _ in range(2)]
        SGS = [RB_.a("SG", [128, 512], F32) for _ in range(2)]
        TD = []
        for d_ in range(2):
            sets = []
            for q_ in range(3):
                sets.append(dict(
                    g=RB_.a("g", [128, 128], F32), eq=RB_.a("eq", [128, 128], F32), ek=RB_.a("ek", [128, 128], F32),
                    er=RB_.a("er", [128, 128], F32), ff=RB_.a("ff", [128, 128], F32), fk=RB_.a("fk", [128, 128], F32),
                    ft=RB_.a("ft", [128, 128], F32),
                    qe=RB_.a("qe", [128, 128], BF16), ke=RB_.a("ke", [128, 128], BF16), kh=RB_.a("kh", [128, 128], BF16),
                    at=RB_.a("at", [128, 128], BF16), dc=RB_.a("dc", [128, 8], F32)))
            TD.append(dict(sets=sets, S=RB_.a("S", [128, 128], F32), Sb=RB_.a("Sb", [128, 128], BF16)))
        assert RB_.off <= SB_END, RB_.off

        for m in range(2):
            dk = 64 if m == 0 else 128
            nchunk = 1 if m == 0 else 2
            cw = 128 // nchunk
            if m == 1:
                for i in range(2):
                    P.dma('sp', TRI[i][:], cd["tri"][i], 'oc')
                    P.dma('sp', STR[i][:], cd["stri"][i], 'oc')
            for h in range(4):
                unit = m * 4 + h
                if m == 0:
                    P.dma('sp', QT[0:64, :], ofm_d[h][0:64, :], 'ul', rkeys=[('of', id(ofm_d[h]))])
                    P.dma('sp', KTb[0:64, :], ofm_d[4 + h][0:64, :], 'ul', rkeys=[('of', id(ofm_d[4 + h]))])
                    P.dma('sp', KTM[:, :, 0:64], otm_d[0].ap().rearrange("t p c -> p t c")[:, :, h * 64:(h + 1) * 64],
                          'ul', rkeys=[('of', id(otm_d[0]))])
                    P.dma('sp', VTM[:], otm_d[1].ap().rearrange("t p c -> p t c")[:, :, h * 128:(h + 1) * 128],
                          'ul', rkeys=[('of', id(otm_d[1]))])
                    P.dma('sp', GATE[:], ofm_d[8 + h][:, 0:2048], 'ul', rkeys=[('of', id(ofm_d[8 + h]))])
                else:
                    P.dma('sp', QT[:], ofm_d[12 + h][:, :], 'ul', rkeys=[('of', id(ofm_d[12 + h]))])
                    for d_ in range(2):
                        P.dma('sp', XF[d_][:], offm_d[d_ * 4 + h][:, :], 'ul', rkeys=[('of', id(offm_d[d_ * 4 + h]))])
                        P.dma('sp', XT[d_][:], otf_d[d_].ap().rearrange("t p c -> p t c")[:, :, h * 128:(h + 1) * 128],
                              'ul', rkeys=[('of', id(otf_d[d_]))])
                    P.dma('sp', VTM[:], otm_d[2].ap().rearrange("t p c -> p t c")[:, :, h * 128:(h + 1) * 128],
                          'ul', rkeys=[('of', id(otm_d[2]))])
                    P.dma('sp', GATE[:], ofm_d[16 + h][:, 0:2048], 'ul', rkeys=[('of', id(ofm_d[16 + h]))])
                P.dma('pool', WOU[:], odwo_d[unit * 128:(unit + 1) * 128, :], 'uw')
                for d_ in range(2):
                    P.memset(TD[d_]['S'][:], 0.0)
                    P.memset(TD[d_]['Sb'][:], 0.0)
                touched = set()
                order = [[16, 17] + list(range(16)), [17, 16] + list(range(15, -1, -1))]

                def feat(s_, d_):
                    t_ = order[d_][s_]
                    T = TD[d_]['sets'][s_ % 3]
                    lat = t_ < 16
                    tk = slice(t_ * 128, (t_ + 1) * 128)
                    bk = 4 * d_
                    pGT = PS[0:dk, bk, 0:128]
                    pR = PS[:, bk + 1, 0:dk]
                    pX = PS[:, bk + 1, 256:256 + dk]
                    pAT = PS[:, bk + 2, 0:128]
                    g = T['g'][:, 0:dk]
                    if m == 0:
                        ar = 0 if d_ == 0 else 32
                        P.mm(pX, AFT[ar:ar + 16, tk], AUP[ar:ar + 16, h * 64:(h + 1) * 64], start=True, stop=False)
                        P.mm(pX, ONE1[ar:ar + 1, :], ABI[ar:ar + 1, h * 64:(h + 1) * 64], start=False, stop=True)
                        yield
                        P.act(T['ff'][:, 0:dk], pX, AF.Exp, scale=-1.0)
                        P.act(T['ff'][:, 0:dk], T['ff'][:, 0:dk], AF.Ln, bias=ONEC[:, 0:1])
                        yield
                        P.ts(g, T['ff'][:, 0:dk], -1.0 / 16.0, None, ALU.mult)
                        ktm = KTM[:, t_, 0:64]
                        ktf = KTb[0:64, tk]
                    else:
                        c0 = h * 128
                        P.act(T['ff'][:], XT[d_][:, t_, :], AF.Exp, scale=-1.0)
                        P.act(T['fk'][:], T['ff'][:], AF.Ln, bias=ONEC[:, 0:1])
                        if lat:
                            P.act(T['ft'][:], XF[d_][:, tk], AF.Exp, scale=-1.0)
                            P.act(T['ft'][:], T['ft'][:], AF.Ln, bias=ONEC[:, 0:1])
                        yield
                        P.tt(T['ff'][:], T['ff'][:], LBB[:, c0:c0 + 128], ALU.mult)
                        if lat:
                            P.tt(T['ft'][:], T['ft'][:], XF[d_][:, tk], ALU.add, eng='pool')
                        yield
                        P.act(T['ff'][:], T['ff'][:], AF.Ln, bias=ONEC[:, 0:1])
                        if lat:
                            P.act(T['ft'][:], T['ft'][:], AF.Exp, scale=-1.0)
                        yield
                        P.tt(g, T['ff'][:], T['fk'][:], ALU.subtract)
                        P.tt(T['fk'][:], T['fk'][:], XT[d_][:, t_, :], ALU.add)
                        if lat:
                            P.ts(T['ft'][:], T['ft'][:], OMLC[:, h:h + 1], None, ALU.mult)
                        yield
                        P.act(T['fk'][:], T['fk'][:], AF.Exp, scale=-1.0)
                        yield
                        P.tt(T['fk'][:], T['fk'][:], OMLB[:, c0:c0 + 128], ALU.mult)
                        ktm = T['fk'][:]
                        ktf = T['ft'][:]
                    yield
                    P.mm(pGT, g, TRI[d_][:], start=True, stop=True)
                    P.mm(pR, STR[d_][:], g, start=True, stop=True)
                    yield
                    P.act(T['er'][:, 0:dk], pR, AF.Exp)
                    if nchunk == 2:
                        ecol = [63, 127] if d_ == 0 else [0, 64]
                    else:
                        ecol = [127] if d_ == 0 else [0]
                    for ci in range(nchunk):
                        P.act(T['dc'][0:dk, ci:ci + 1], PS[0:dk, bk, ecol[ci]:ecol[ci] + 1], AF.Exp)
                    if lat:
                        P.act(T['eq'][0:dk, :], pGT, AF.Exp)
                        P.act(T['ek'][0:dk, :], pGT, AF.Exp, scale=-1.0)
                    yield
                    P.tt(T['kh'][:, 0:dk], ktm, T['er'][:, 0:dk], ALU.mult)
                    if lat:
                        P.tt(T['qe'][0:dk, :], QT[0:dk, tk], T['eq'][0:dk, :], ALU.mult)
                        P.tt(T['ke'][0:dk, :], ktf, T['ek'][0:dk, :], ALU.mult, eng='pool')
                        yield
                        P.mm(pAT, T['ke'][0:dk, :], T['qe'][0:dk, :], start=True, stop=True)
                        yield
                        P.tt(T['at'][:], pAT, TRI[d_][:], ALU.mult)

                def rec_chunk(s_, d_, ii):
                    t_ = order[d_][s_]
                    T = TD[d_]['sets'][s_ % 3]
                    St = TD[d_]
                    lat = t_ < 16
                    tk = slice(t_ * 128, (t_ + 1) * 128)
                    bk = 4 * d_
                    ci = ii if d_ == 0 else nchunk - 1 - ii
                    rows = slice(ci * cw, (ci + 1) * cw)
                    pB = PS[0:dk, bk + 2, 256 + 128 * ci:384 + 128 * ci]
                    pO = PS[:, bk + 3, 0:128]
                    if lat and ii == 0:
                        P.mm(pO, VTM[:, t_, :], T['at'][:], start=True, stop=False)
                    if lat:
                        P.mm(PS[:, bk + 3, ci * cw:(ci + 1) * cw], St['Sb'][0:dk, :], T['qe'][0:dk, rows],
                             start=False, stop=(ii == nchunk - 1))
                    P.mm(pB, T['kh'][rows, 0:dk], VTM[rows, t_, :], start=True, stop=True)
                    P.stt(St['Sb'][0:dk, :], St['S'][0:dk, :], T['dc'][0:dk, ci:ci + 1], pB, ALU.mult, ALU.add)
                    P.stt(St['S'][0:dk, :], St['S'][0:dk, :], T['dc'][0:dk, ci:ci + 1], pB, ALU.mult, ALU.add)
                    if lat and ii == nchunk - 1:
                        if t_ in touched:
                            P.tt(OT[:, tk], pO, OT[:, tk], ALU.add)
                        else:
                            P.copy(OT[:, tk], pO, eng='dve')
                            touched.add(t_)

                prog = {'feat': [0, 0], 'rec': 0}

                def featseq(d_):
                    for s_ in range(18):
                        while prog['rec'] < s_ - 2:
                            yield
                        yield from feat(s_, d_)
                        prog['feat'][d_] = s_ + 1
                        yield

                def rec_all():
                    for s_ in range(18):
                        while min(prog['feat']) <= s_:
                            yield
                        for ii in range(nchunk):
                            for d_ in range(2):
                                rec_chunk(s_, d_, ii)
                                yield
                        prog['rec'] = s_ + 1

                P.fixed = ('pe', 'act', 'pool', 'sp')
                run_threads([featseq(0), featseq(1), rec_all()])
                P.fixed = False
                def readout_blk(bi):
                    bs = slice(bi * 512, (bi + 1) * 512)
                    SGv, SQv, RSv = SGS[bi % 2], SQBS[bi % 2], RSS[bi % 2]
                    P.act(SGv[:], GATE[:, bs], AF.Exp, scale=-1.0)
                    yield
                    P.act(SGv[:], SGv[:], AF.Ln, bias=ONEC[:, 0:1])
                    yield
                    P.act(SGv[:], SGv[:], AF.Exp, scale=-1.0)
                    yield
                    if m == 1:
                        P.tt(OT[:, bs], OT[:, bs], SGv[:], ALU.mult)
                    else:
                        P.tt(SGv[:], SGv[:], GATE[:, bs], ALU.mult, eng='pool')
                    yield
                    P.act(SQv[:], OT[:, bs], AF.Square)
                    yield
                    P.mm(psb(6 + bi % 2), ONES8[:], SQv[:], start=True, stop=True)
                    yield
                    P.act(RSv[:], psb(6 + bi % 2), AF.Ln, bias=EPSC[LN_EPS][:, 0:1])
                    yield
                    P.act(RSv[:], RSv[:], AF.Exp, scale=-0.5)
                    yield
                    P.tt(RSv[:], OT[:, bs], RSv[:], ALU.mult)
                    yield
                    if m == 0:
                        P.stt(YU[:, bs], RSv[:], GNC[:, 0:1], SGv[:], ALU.mult, ALU.mult)
                    else:
                        P.ts(perm_blk(YU[:, 0:2048], bi), RSv[:].rearrange("p (c r) -> p c r", r=32), GNC[:, 1:2], None, ALU.mult)
                run_threads([readout_blk(0), readout_blk(1)])
                run_threads([readout_blk(2), readout_blk(3)])
                for bi in range(4):
                    t0, n = BLKS[bi]
                    rhs = YU[:, t0:t0 + n]
                    for kf in range(8):
                        py = psb(6 + pctr[0] % 2, n)
                        pctr[0] += 1
                        P.mm(py, WOU[:, kf * 128:(kf + 1) * 128], rhs, start=True, stop=True)
                        P.stt(X[:, kf, t0:t0 + n], py, mcol(1, 5, kf, b), X[:, kf, t0:t0 + n], ALU.mult, ALU.add)
        post_ln(1, 1, 4)

    lctr = [0]

    def load_x(b):
        for tt_ in range(18):
            stg = STG[lctr[0] % 2]
            key = 'ld%d' % (lctr[0] % 2)
            lctr[0] += 1
            src = x_d[b, tt_ * 128:(tt_ + 1) * 128, :] if tt_ < 16 else ctx_d[b, (tt_ - 16) * 128:(tt_ - 15) * 128, :]
            P.dma('sp', stg[:], src, key)
            for half in range(2):
                bank = 4 + half
                for kk in range(4):
                    k = half * 4 + kk
                    P.tr(PS[:, bank, kk * 128:(kk + 1) * 128], stg[:, k * 128:(k + 1) * 128], IDF[:])
                dst = X[:, half * 4:(half + 1) * 4, tt_ * 128:(tt_ + 1) * 128]
                srcp = PS[:, bank, :].rearrange("p (k t) -> p k t", k=4)
                P.copy(dst, srcp, eng=('act' if half == 0 else 'dve'))

    def store_x(b, ntile=16):
        for tt_ in range(ntile):
            stg = STG[lctr[0] % 2]
            key = 'st%d' % (lctr[0] % 2)
            lctr[0] += 1
            for half in range(2):
                bank = 4 + half
                for kk in range(4):
                    k = half * 4 + kk
                    P.tr(PS[:, bank, kk * 128:(kk + 1) * 128], X[:, k, tt_ * 128:(tt_ + 1) * 128], IDF[:])
                if half == 0:
                    P.copy(stg[:, 0:512], PS[:, bank, :], eng='act')
                else:
                    P.copy(stg[:, 512:1024], PS[:, bank, :], eng='dve')
            P.dma('sp', out_d[b, tt_ * 128:(tt_ + 1) * 128, :], stg[:], key)
        return ['st0', 'st1']

    stages = ['load', 'ffn1', 'mix0', 'ffn2', 'l1ffn1', 'mix1', None]
    sub = None
    if stop == 'ffnx2':
        sub = stop
        stop = 'ffn1'
    if stop in ('setup', 'mix0a', 'mix0b'):
        sub = stop
        stop = 'mix0'
    lim = stages.index(stop)
    def hy_chain():
        yield from hy_setup("L", 2048)
        yield from hy_setup("C", 256)
    if lim >= 2:
        run_threads([mod_gen(), hy_chain()])
    else:
        run_threads([mod_gen()])
    fkeys = []
    for b in range(nb):
        load_x(b)
        if lim >= 1:
            ffn(0, 0, b, 5)
        if sub == 'ffnx2':
            ffn(0, 1, b, 5)
        if lim >= 2 and sub != 'setup':
            even_mixer(b, sub)
        if lim >= 3:
            ffn(0, 1, b, 5)
        if lim >= 4:
            ffn(1, 0, b, 5)
        if lim >= 5:
            odd_mixer(b)
        if lim >= 6:
            ffn(1, 1, b, 4)
        fkeys = store_x(b)

    if SCHED:
        P.schedule()
    P.emit(fkeys)
    nc_ctx.__exit__(None, None, None)
    return nc, P


WEIGHT_KEYS = ["mod_w", "mod_b", "ffn_w_in", "ffn_w_out", "ln_g", "ln_b"]


def make_inputs(inputs, nb=2):
    f32 = np.float32
    ncores = 16 // nb

    def A(k, shape=None):
        a = np.ascontiguousarray(inputs[k], dtype=f32)
        return a.reshape(shape) if shape is not None else a
    common = {k: A(k) for k in WEIGHT_KEYS}
    common["c_ctx"] = A("c_ctx", (1, D))
    common["ev_w_in"] = A("ev_w_in", (D, 2048))
    common["ev_w_out"] = A("ev_w_out", (D, D))
    common["hy_conv_w"] = A("hy_conv_w", (3, 1536))
    common["hy_conv_b"] = A("hy_conv_b", (1, 1536))
    common["hy_w1"] = A("hy_w1", (33, 64))
    common["hy_b1"] = A("hy_b1", (64, 1))
    common["hy_w2"] = A("hy_w2", (64, 64))
    common["hy_b2"] = A("hy_b2", (64, 1))
    common["hy_w3"] = A("hy_w3", (64, 2048))
    common["hy_freq"] = A("hy_freq", (64, 1))
    common["hy_skip"] = A("hy_skip", (2, 512))
    common["od_w_in"] = A("od_w_in", (D, 4128))
    common["od_w_out"] = A("od_w_out", (D, D))
    common["gla_a_up"] = A("gla_a_up", (2, 16, 256))
    common["gla_a_b"] = A("gla_a_b", (2, 256))
    common["gla_norm_g"] = A("gla_norm_g", (128, 1))
    common["hg_lb"] = A("hg_lb", (2, 512))
    common["hg_norm_g"] = A("hg_norm_g", (128, 1))
    common.update(get_consts())
    maps = []
    for i in range(ncores):
        m = dict(common)
        m["x"] = np.ascontiguousarray(inputs["x"][i * nb:(i + 1) * nb], dtype=f32)
        m["ctx"] = np.ascontiguousarray(inputs["ctx"][i * nb:(i + 1) * nb], dtype=f32)
        m["c"] = np.ascontiguousarray(inputs["c"][i * nb:(i + 1) * nb], dtype=f32)
        maps.append(m)
    return maps


def kernel(**inputs):
    nc, P = build()
    maps = make_inputs(inputs)
    res = run_bass_kernel_spmd(nc, maps, core_ids=list(range(8)))
    return np.concatenate([np.asarray(r["out"], dtype=np.float32) for r in res.results], axis=0)
```
